# Optimizing a Trainium2 kernel written in Bass

```python
import math
import jax, jax.numpy as jnp
from jax import lax
import numpy as np

D_MODEL = 1024
BATCH = 32
SEQ = 256
DEPTH = 1
DEC_BATCH = 4
DEC_SEQ = 1024
PAST_LEN = 512

GRID_W = 64
CHUNK = 128
N_RET_HEADS = 4
RET_DK = D_MODEL // 8
RET_DV = D_MODEL // 4
RET_QK = N_RET_HEADS * RET_DK
RET_V = N_RET_HEADS * RET_DV
N_SG_GROUPS = 4
SG_WIDTH = D_MODEL
SG_GROUP = SG_WIDTH // N_SG_GROUPS
D_FF = ((8 * D_MODEL // 3 + 127) // 128) * 128
N_MOD = 9
ROPE_THETA = 10000.0
EPS = 1e-6
IN_SIZES = (RET_QK, RET_QK, RET_V, RET_V, SG_WIDTH, SG_WIDTH, 2 * D_MODEL)
IN_WIDTH = sum(IN_SIZES)
IN_SPLITS = tuple(int(s) for s in np.cumsum(IN_SIZES)[:-1])

kernel_name = "hybrid_retention_gmlp_diffusion_step"


def _rms(x, g):
    xf = x.astype(jnp.float32)
    y = xf * lax.rsqrt(jnp.mean(xf * xf, axis=-1, keepdims=True) + EPS)
    return (y * g.astype(jnp.float32)).astype(x.dtype)


def _swiglu(h, w1, w2):
    a, b = jnp.split(h @ w1, 2, axis=-1)
    return (jax.nn.silu(a) * b) @ w2


def _axial_rope(L):
    rows = L // GRID_W
    r = jnp.repeat(jnp.arange(rows), GRID_W).astype(jnp.float32)
    col = (jnp.arange(rows * GRID_W) % GRID_W).astype(jnp.float32)
    nf = RET_DK // 4
    freqs = ROPE_THETA ** (-jnp.arange(nf, dtype=jnp.float32) / nf)
    ang = jnp.concatenate([r[:, None] * freqs, col[:, None] * freqs], axis=-1)
    return jnp.cos(ang), jnp.sin(ang)


def _apply_rope(x, cos, sin):
    half = RET_DK // 2
    x1, x2 = x[..., :half], x[..., half:]
    return jnp.concatenate([x1 * cos - x2 * sin, x1 * sin + x2 * cos], axis=-1)


def _retention_chunkwise(q, k, v, log_g, S0):
    B, H, L, _ = q.shape
    n = L // CHUNK
    idx = jnp.arange(CHUNK, dtype=jnp.float32)
    diff = idx[:, None] - idx[None, :]
    causal = diff >= 0
    dmask = jnp.where(causal, jnp.exp(log_g[:, None, None] * jnp.where(causal, diff, 0.0)), 0.0)
    q_dec = jnp.exp(log_g[:, None] * (idx + 1.0))[..., None]
    k_dec = jnp.exp(log_g[:, None] * (CHUNK - 1.0 - idx))[..., None]
    chunk_dec = jnp.exp(log_g * CHUNK)[:, None, None]

    def to_chunks(t):
        return t.reshape(B, H, n, CHUNK, t.shape[-1]).transpose(2, 0, 1, 3, 4)

    def step(S, qkv):
        qc, kc, vc = qkv
        scores = jnp.einsum('bhid,bhjd->bhij', qc, kc) * dmask
        o = (jnp.einsum('bhij,bhje->bhie', scores, vc)
             + jnp.einsum('bhid,bhde->bhie', qc * q_dec, S))
        S = S * chunk_dec + jnp.einsum('bhjd,bhje->bhde', kc * k_dec, vc)
        return S, o

    S, o = lax.scan(step, S0, (to_chunks(q), to_chunks(k), to_chunks(v)))
    o = o.transpose(1, 2, 0, 3, 4).reshape(B, H, L, v.shape[-1])
    return o, S


def _mixer(h, rope, S0f, S0b, w_in, decay_logit, g_ret, w_ret_br, g_sg, b_sg, w_sp, b_sp, w_sg_br, w_out):
    B, L, _ = h.shape
    z = h @ w_in
    q, k, v, g, u, vs, gates = jnp.split(z, IN_SPLITS, axis=-1)

    def heads(t, d):
        return t.reshape(B, L, N_RET_HEADS, d).transpose(0, 2, 1, 3).astype(jnp.float32)
    qh = heads(q, RET_DK) * (RET_DK ** -0.5)
    kh = heads(k, RET_DK)
    vh = heads(v, RET_DV)
    if rope is not None:
        qh = _apply_rope(qh, *rope)
        kh = _apply_rope(kh, *rope)
    log_g = jax.nn.log_sigmoid(decay_logit.astype(jnp.float32))
    of, Sf = _retention_chunkwise(qh, kh, vh, log_g[0], S0f.astype(jnp.float32))
    flip = lambda t: jnp.flip(t, axis=2)
    ob, Sb = _retention_chunkwise(flip(qh), flip(kh), flip(vh), log_g[1], S0b.astype(jnp.float32))
    r = of + flip(ob)
    mu = jnp.mean(r, axis=-1, keepdims=True)
    rc = r - mu
    r = rc * lax.rsqrt(jnp.mean(rc * rc, axis=-1, keepdims=True) + EPS)
    r = r.transpose(0, 2, 1, 3).reshape(B, L, RET_V) * g_ret.astype(jnp.float32)
    ret_out = (jax.nn.silu(g.astype(jnp.float32)) * r).astype(h.dtype) @ w_ret_br

    u = jax.nn.gelu(u)
    vsf = jax.nn.gelu(vs).astype(jnp.float32)
    vm = jnp.mean(vsf, axis=-1, keepdims=True)
    vc_ = vsf - vm
    vsn = vc_ * lax.rsqrt(jnp.mean(vc_ * vc_, axis=-1, keepdims=True) + EPS)
    vsn = (vsn * g_sg.astype(jnp.float32) + b_sg.astype(jnp.float32)).astype(h.dtype)
    n = L // CHUNK
    vsn = vsn.reshape(B, n, CHUNK, N_SG_GROUPS, SG_GROUP)
    sp = jnp.einsum('gij,bnjgc->bnigc', w_sp, vsn) + b_sp.T[:, :, None]
    sg_out = (u * sp.reshape(B, L, SG_WIDTH)) @ w_sg_br

    gate_r, gate_s = jnp.split(jax.nn.sigmoid(gates), 2, axis=-1)
    y = (gate_r * ret_out + gate_s * sg_out) @ w_out
    return y, Sf, Sb


def _layer(x, mod, rope, S0f, S0b, g_norm, w_ffn1_in, w_ffn1_out, w_in, decay_logit, g_ret, w_ret_br,
           g_sg, b_sg, w_sp, b_sp, w_sg_br, w_out, w_ffn2_in, w_ffn2_out):
    mo = lambda i: mod[:, i][:, None, :]
    hh = _rms(x, g_norm[0]) * (1 + mo(1)) + mo(0)
    x = x + 0.5 * mo(2) * _rms(_swiglu(hh, w_ffn1_in, w_ffn1_out), g_norm[1])
    hh = _rms(x, g_norm[2]) * (1 + mo(4)) + mo(3)
    y, Sf, Sb = _mixer(hh, rope, S0f, S0b, w_in, decay_logit, g_ret, w_ret_br, g_sg, b_sg, w_sp, b_sp, w_sg_br, w_out)
    x = x + mo(5) * _rms(y, g_norm[3])
    hh = _rms(x, g_norm[4]) * (1 + mo(7)) + mo(6)
    x = x + 0.5 * mo(8) * _rms(_swiglu(hh, w_ffn2_in, w_ffn2_out), g_norm[5])
    return x, Sf, Sb


def setup_inputs(seed: int = 0) -> dict:
    key = jax.random.key(seed)
    ks = jax.random.split(key, 24)
    nrm = lambda k, s, sc=1.0: jax.random.normal(k, s, jnp.float32) * sc
    D = D_MODEL
    base_logit = jnp.log(2.0 ** (5.0 + jnp.arange(N_RET_HEADS, dtype=jnp.float32)) - 1.0)
    return {
        "x_prompt": nrm(ks[0], (BATCH, SEQ, D)),
        "x_sample": nrm(ks[1], (DEC_BATCH, DEC_SEQ, D)),
        "state_ret_fwd": nrm(ks[2], (DEC_BATCH, DEPTH, N_RET_HEADS, RET_DK, RET_DV), 0.3),
        "state_ret_bwd": nrm(ks[3], (DEC_BATCH, DEPTH, N_RET_HEADS, RET_DK, RET_DV), 0.3),
        "c": nrm(ks[4], (DEC_BATCH, D)),
        "c_ctx": nrm(ks[5], (D,)),
        "w_ada": nrm(ks[6], (DEPTH, D, N_MOD * D), 0.3 * D ** -0.5),
        "b_ada": nrm(ks[7], (DEPTH, N_MOD * D), 0.1),
        "g_norm": 1.0 + nrm(ks[8], (DEPTH, 6, D), 0.1),
        "w_ffn1_in": nrm(ks[9], (DEPTH, D, 2 * D_FF), D ** -0.5),
        "w_ffn1_out": nrm(ks[10], (DEPTH, D_FF, D), D_FF ** -0.5),
        "w_in": nrm(ks[11], (DEPTH, D, IN_WIDTH), D ** -0.5),
        "ret_decay_logit": base_logit[None, None, :] + nrm(ks[12], (DEPTH, 2, N_RET_HEADS), 0.1),
        "g_ret": 1.0 + nrm(ks[13], (DEPTH, RET_V), 0.1),
        "w_ret_br": nrm(ks[14], (DEPTH, RET_V, D), RET_V ** -0.5),
        "g_sg": 1.0 + nrm(ks[15], (DEPTH, SG_WIDTH), 0.1),
        "b_sg": nrm(ks[16], (DEPTH, SG_WIDTH), 0.02),
        "w_sp": nrm(ks[17], (DEPTH, N_SG_GROUPS, CHUNK, CHUNK), CHUNK ** -0.5),
        "b_sp": 1.0 + nrm(ks[18], (DEPTH, N_SG_GROUPS, CHUNK), 0.1),
        "w_sg_br": nrm(ks[19], (DEPTH, SG_WIDTH, D), SG_WIDTH ** -0.5),
        "w_out": nrm(ks[20], (DEPTH, D, D), D ** -0.5),
        "w_ffn2_in": nrm(ks[21], (DEPTH, D, 2 * D_FF), D ** -0.5),
        "w_ffn2_out": nrm(ks[22], (DEPTH, D_FF, D), D_FF ** -0.5),
    }


def reference(x_prompt, x_sample, state_ret_fwd, state_ret_bwd, c, c_ctx, w_ada, b_ada, g_norm,
              w_ffn1_in, w_ffn1_out, w_in, ret_decay_logit, g_ret, w_ret_br, g_sg, b_sg, w_sp, b_sp,
              w_sg_br, w_out, w_ffn2_in, w_ffn2_out):
    yp = x_prompt
    ys = x_sample
    Bp = x_prompt.shape[0]
    Bs = x_sample.shape[0]
    rope = _axial_rope(x_sample.shape[1])
    zero_state = jnp.zeros((Bp, N_RET_HEADS, RET_DK, RET_DV), jnp.float32)
    new_f, new_b = [], []
    for l in range(DEPTH):
        lw = (g_norm[l], w_ffn1_in[l], w_ffn1_out[l], w_in[l], ret_decay_logit[l], g_ret[l], w_ret_br[l],
              g_sg[l], b_sg[l], w_sp[l], b_sp[l], w_sg_br[l], w_out[l], w_ffn2_in[l], w_ffn2_out[l])
        mod_ctx = (jax.nn.silu(c_ctx)[None, :] @ w_ada[l] + b_ada[l]).reshape(1, N_MOD, D_MODEL)
        mod_lat = (jax.nn.silu(c) @ w_ada[l] + b_ada[l]).reshape(Bs, N_MOD, D_MODEL)
        yp, Sf, Sb = _layer(yp, mod_ctx, None, zero_state, zero_state, *lw)
        new_f.append(Sf)
        new_b.append(Sb)
        ys, _, _ = _layer(ys, mod_lat, rope, state_ret_fwd[:, l], state_ret_bwd[:, l], *lw)
    new_state_ret_fwd = jnp.stack(new_f, axis=1).astype(x_prompt.dtype)
    new_state_ret_bwd = jnp.stack(new_b, axis=1).astype(x_prompt.dtype)
    return (yp, ys, new_state_ret_fwd, new_state_ret_bwd)
```

```python
import numpy as np
from contextlib import ExitStack
import concourse.bass as bass
import concourse.mybir as mybir
from concourse.bass_utils import run_bass_kernel_spmd

F32 = mybir.dt.float32
BF16 = mybir.dt.bfloat16
AF = mybir.ActivationFunctionType
ALU = mybir.AluOpType

D = 1024
DFF = 2816
NTOK = 1536
EPS = 1e-6
import os
NSLOT = int(os.environ.get('K_NSLOT', '6'))
K_WIN = [int(v) for v in os.environ.get('K_WIN', '56,20,20').split(',')]
K_SIMONLY = os.environ.get('K_SIMONLY') == '1'
K_SAFT1 = os.environ.get('K_SAFT1', '1') == '1'
K_HT1 = os.environ.get('K_HT1', '1') == '1'
K_SCR = os.environ.get('K_SCR', '0') == '1'
K_PREN = os.environ.get('K_PREN', '1') == '1'
K_TBLPF = os.environ.get('K_TBLPF', '1') == '1'
K_DB = os.environ.get('K_DB', '1') == '1'
K_MODSPREAD = os.environ.get('K_MODSPREAD', '1') == '1'
MODPTS = (1, 3, 5) if K_MODSPREAD else ()

O_ID, O_A, O_B, O_R127, O_RP = 0, 128, 256, 384, 512
O_RP1, O_R128, O_EPS, O_ONE, O_FLAG, O_ZERO = 640, 641, 642, 643, 644, 645
O_GN, O_BADA, O_GRET, O_C2 = 648, 696, 768, 776
O_GSG, O_BSG = 792, 800
NS = 808
R_LG, R_GSG, R_BSG, R_BSP = 0, 8, 1032, 2056
NR = 2568


class Op:
    __slots__ = ("eng", "fn", "deps", "inc", "val", "sem", "waits", "is_dma", "tag", "idx", "cost", "fin", "bw")


class Sched:
    ENGS = ("pe", "act", "dve", "pool", "sp")

    def __init__(self):
        self.ops = {e: [] for e in self.ENGS}
        self.lw = {}
        self.rd = {}
        self.gdeps = []
        self.dma_cnt = {}
        self.out_dmas = []
        self.tag = "setup"
        self.alias = {}

    def add(self, eng, fn, rd=(), wr=(), dma_key=None):
        op = Op()
        op.eng, op.fn, op.inc, op.val, op.sem, op.waits = eng, fn, False, 0, None, []
        op.is_dma = dma_key is not None
        op.tag = self.tag
        op.idx = len(self.ops[eng])
        op.cost = getattr(fn, "cost", 500.0)
        op.bw = getattr(fn, "bw", 0.0)
        rd = tuple(k2 for k in rd for k2 in self.alias.get(k, (k,)))
        wr = tuple(k2 for k in wr for k2 in self.alias.get(k, (k,)))
        deps = set(self.gdeps)
        for b in rd:
            w = self.lw.get(b)
            if w is not None:
                deps.add(w)
        for b in wr:
            w = self.lw.get(b)
            if w is not None:
                deps.add(w)
            deps.update(self.rd.get(b, ()))
        op.deps = deps
        for b in rd:
            self.rd.setdefault(b, []).append(op)
        for b in wr:
            self.lw[b] = op
            self.rd[b] = []
        if op.is_dma:
            self.dma_cnt[dma_key] = self.dma_cnt.get(dma_key, 0) + 1
            op.sem = dma_key
            op.val = 16 * self.dma_cnt[dma_key]
        self.ops[eng].append(op)
        return op

    def barrier(self):
        g = []
        for e in ("pe", "act", "dve"):
            if self.ops[e]:
                g.append(self.ops[e][-1])
        self.gdeps = g

    def reorder(self, window):
        rem = {e: list(self.ops[e]) for e in self.ENGS}
        free = {e: 0.0 for e in self.ENGS}
        new = {e: [] for e in self.ENGS}
        for e in self.ENGS:
            for op in self.ops[e]:
                op.fin = None
        left = sum(len(v) for v in rem.values())
        bw_free = 0.0
        while left:
            best = None
            for e in self.ENGS:
                r = rem[e]
                if not r:
                    continue
                w = window.get(e, 1)
                for i in range(min(w, len(r))):
                    op = r[i]
                    rt = free[e]
                    ok = True
                    for d in op.deps:
                        if d.fin is None:
                            ok = False
                            break
                        lat = 60.0 if d.eng == e else 180.0
                        if d.fin + lat > rt:
                            rt = d.fin + lat
                    if not ok:
                        continue
                    cand = (rt, i, e)
                    if best is None or (rt, i) < (best[0], best[1]):
                        best = (rt, i, e)
                    if rt <= free[e]:
                        break
            assert best is not None, "scheduler deadlock"
            rt, i, e = best
            op = rem[e].pop(i)
            if op.bw:
                rt = max(rt, bw_free)
                bw_free = rt + op.bw
            op.fin = rt + op.cost
            free[e] = op.fin
            new[e].append(op)
            left -= 1
        for e in self.ENGS:
            self.ops[e] = new[e]
            for i, op in enumerate(new[e]):
                op.idx = i
        self.est_ns = max(free.values())

    def resolve(self):
        for e in self.ENGS:
            for op in self.ops[e]:
                last = {}
                for d in op.deps:
                    if d.is_dma:
                        continue
                    if d.eng == "pe" and op.eng == "pe":
                        continue
                    if d.eng not in last or d.idx > last[d.eng].idx:
                        last[d.eng] = d
                for d in last.values():
                    d.inc = True
                op.deps = set(x for x in op.deps if x.is_dma or last.get(x.eng) is x)
        for e in ("pe", "act", "dve"):
            cnt = 0
            for op in self.ops[e]:
                if op.inc:
                    cnt += 1
                    op.val = cnt
        for e in self.ENGS:
            waited = {}
            for op in self.ops[e]:
                need = {}
                for d in op.deps:
                    if d.is_dma:
                        key = ("dma", d.sem)
                    else:
                        if d.eng == "pe" and e == "pe":
                            continue
                        key = ("eng", d.eng)
                    if d.val > need.get(key, 0):
                        need[key] = d.val
                for key, v in need.items():
                    if v > waited.get(key, 0):
                        waited[key] = v
                        op.waits.append((key, v))


def build_program():
    nc = bass.Bass("TRN2", target_bir_lowering=False)

    def dram(name, shape, kind="ExternalInput"):
        return nc.dram_tensor(name, shape, F32, kind=kind).ap()

    x_in = dram("x_in", [NTOK, D])
    small_d = dram("small", [128, NS])
    rows_d = dram("rows", [1, NR])
    rope_d = dram("rope", [3, 2, 128, 4, 64])
    s0_d = dram("s0", [2, 128, 1024])
    w_ada = dram("w_ada", [D, 9 * D])
    w1 = [dram("w_ffn1_in", [D, 2 * DFF]), dram("w_ffn2_in", [D, 2 * DFF])]
    w2 = [dram("w_ffn1_out", [DFF, D]), dram("w_ffn2_out", [DFF, D])]
    w_in = dram("w_in", [D, 7168])
    w_ret = dram("w_ret_br", [D, D])
    w_sg = dram("w_sg_br", [D, D])
    w_o = dram("w_out", [D, D])
    w_sp = dram("w_sp", [4, 128, 128])
    y_out = dram("y_out", [NTOK, D], kind="ExternalOutput")
    st_out = dram("st_out", [3, 2, 2, 128, 1024], kind="ExternalOutput")

    S = Sched()
    es = ExitStack()
    with es:
        def sb(name, shape, dt=F32):
            return es.enter_context(nc.sbuf_tensor("sb_" + name, shape, dt))

        x_all = sb("x_all", [128, 8, NTOK])
        Wt = [sb(f"W{i}", [128, 8, 512], BF16) for i in range(min(NSLOT, 4) if K_SIMONLY else NSLOT)]
        while len(Wt) < NSLOT:
            Wt.append(Wt[len(Wt) % 4])
        small = sb("small", [128, NS])
        lgt = sb("lgt", [128, 8])
        lg = sb("lg", [128, 8])
        Bt = sb("Bt", [128, 8, 128])
        MT = sb("MT", [128, 512])
        kdf = sb("kdf", [128, 512])
        kdb = sb("kdb", [128, 512])
        Dq = sb("Dq", [128, 8, 128], BF16)
        identB = sb("identB", [128, 128], BF16)
        onesB = sb("onesB", [128, 128], BF16)
        wsp_n = sb("wsp_n", [128, 4, 128], BF16)
        wspT = sb("wspT", [128, 4, 128], BF16)
        cc = sb("cc", [128, 32])
        dcol = sb("dcol", [128, 8])
        cT = sb("cT", [128, 8, 2], BF16)
        mod = sb("mod", [128, 72, 2])
        Apre = sb("Apre", [128, 3, 8, 2])
        Gpost = sb("Gpost", [128, 3, 8, 2])
        SaftF = [sb(f"SaftF{i}", [128, 1024]) for i in range(1 if K_SAFT1 else 2)]
        SaftB = [sb(f"SaftB{i}", [128, 1024]) for i in range(1 if K_SAFT1 else 2)]
        if K_SAFT1:
            SaftF, SaftB = SaftF * 2, SaftB * 2
        def skey(dirn, i):
            return f"Saft{dirn}" + ("" if K_SAFT1 else str(i))
        tmpf = [sb(f"tmpf{i}", [128, 512]) for i in range(3)]
        rstd_t = sb("rstd_t", [128, 512])
        cs_t = sb("cs_t", [128, 4, 64])
        sn_t = sb("sn_t", [128, 4, 64])
        stat = sb("stat", [128, 48])
        sqb = [sb(f"sqb{i}", [128, 512], BF16) for i in range(2)]
        hTs = [sb("hTa", [128, 8, 512], BF16)]
        hTs.append(hTs[0] if K_HT1 else sb("hTb", [128, 8, 512], BF16))
        cur = {"i": 0}
        rgT = sb("rgT", [128, 8, 512], BF16)
        Sbbf = sb("Sbbf", [128, 4, 1024], BF16)
        Sfbf = sb("Sfbf", [128, 1024], BF16)
        rg = sb("rg", [128, 1024], BF16)
        UB = sb("UB", [128, 11264], BF16)
        UF = sb("UF", [128, 4096])
        VS2 = sb("VS2", [128, 1024])[:, :]
        vsn2 = sb("vsn2", [128, 1024], BF16)[:, :]
        hidden = UB[:, :].rearrange("p (j t) -> p j t", t=512)
        v_tm = UB[:, 0:4096].rearrange("p (c e) -> p c e", e=1024)
        kT = UB[:, 4096:6144].rearrange("p (c e) -> p c e", e=512)
        kfw = UB[:, 6144:8192].rearrange("p (c e) -> p c e", e=512)
        qrot = UB[:, 8192:8704]
        krot = UB[:, 8704:9216]
        qT3 = UB[:, 9216:10752].rearrange("p (a e) -> p a e", e=512)
        Pm = UB[:, 10752:11264]
        su = UB[:, 0:4096].rearrange("p (j t) -> p j t", t=512)
        ypre = UB[:, 4096:8192].rearrange("p (j t) -> p j t", t=512)
        vsn = UB[:, 8192:9216]
        kbw = UB[:, 10240:10752]
        VS = UB[:, 9216:11264].bitcast(F32)
        y_sb = UF[:, :].rearrange("p (j t) -> p j t", t=512)
        u_act = y_sb
        qf32 = UF[:, 0:512]
        kf32 = UF[:, 512:1024]
        rtmp = UF[:, 1024:2048].rearrange("p (a h e) -> p a h e", a=4, h=4)
        r_t = UF[:, 2048:3072]
        sgl = UF[:, 3072:4096]
        xs = [UF[:, 0:1024], UF[:, 1024:2048], UF[:, 2048:3072], UF[:, 3072:4096]]

        AL = S.alias
        def UFk(e):
            return (f"UF{e}a", f"UF{e}b")
        for j in range(22):
            AL[f"hid{j}"] = (f"UB{j}",)
        for c_ in range(4):
            AL[f"v{c_}"] = (f"UB{2 * c_}", f"UB{2 * c_ + 1}")
            AL[f"kT{c_}"] = (f"UB{8 + c_}",)
            AL[f"kfw{c_}"] = (f"UB{12 + c_}",)
        AL["qrot"], AL["krot"], AL["qT0"], AL["qT1"], AL["qT2"], AL["Pm"] = ("UB16",), ("UB17",), ("UB18",), ("UB19",), ("UB20",), ("UB21",)
        AL["vsn"] = ("UB16", "UB17")
        AL["VS"] = ("UB18", "UB19", "UB20", "UB21")
        AL["VS2"] = ("VS2_0", "VS2_1", "VS2_2", "VS2_3")
        for e_ in range(8):
            AL[f"sub{e_}"] = (f"UB{e_}",)
            AL[f"ypre{e_}"] = (f"UB{8 + e_}",)
            AL[f"ysb{e_}"] = UFk(e_)
        AL["qf32"], AL["kf32"] = UFk(0), UFk(1)
        for i_ in range(4):
            AL[f"rtmp{i_}"] = (f"UF{2 + i_ // 2}{'ab'[i_ % 2]}",)
            AL[f"r_t{i_}"] = (f"UF{4 + i_ // 2}{'ab'[i_ % 2]}",)
            AL[f"xs{i_}"] = UFk(2 * i_) + UFk(2 * i_ + 1)
        AL["r_t"] = UFk(4) + UFk(5)
        AL["sgl"] = UFk(6) + UFk(7)

        ps = es.enter_context(nc.psum_tensor("ps", [128, 8, 512], F32))
        rr = [0]

        NRR = 7

        def pb():
            b = rr[0] % NRR
            rr[0] += 1
            return b

        def pb2():
            while (rr[0] % NRR) % 2 or (rr[0] % NRR) + 1 >= NRR:
                rr[0] += 1
            b = rr[0] % NRR
            rr[0] += 2
            return b

        PSN = 7

        def c(o, n=1):
            return small[:, o:o + n]

        def _free(ap):
            n = 1
            for d in ap.shape[1:]:
                n *= d
            return n

        def R(name, *a, **kw):
            fn = lambda e: getattr(e, name)(*a, **kw)
            if name == "matmul":
                fn.cost = max(64, _free(a[0])) / 2.4 + 8.0
            elif name == "transpose":
                fn.cost = 120.0
            elif name == "dma_start":
                fn.bw = _free(kw["out"]) * 128 * (4 if kw["in_"].dtype == F32 else 2) / 330.0
                fn.cost = 2000.0 + fn.bw
            else:
                o = kw.get("out", a[0] if a else None)
                n = _free(o) if o is not None else 512
                fn.cost = (250.0 + n / 1.2) if name == "activation" else (150.0 + n / 0.96)
            return fn

        def A(fn, rd, wr):
            return S.add("act", fn, rd, wr)

        def V(fn, rd, wr):
            return S.add("dve", fn, rd, wr)

        def T(fn, rd, wr):
            return S.add("pe", fn, rd, wr)

        def LD(fn, key, wr, rd=()):
            return S.add("sp", fn, rd, wr, dma_key=key)

        slot_rr = [0]

        scratch = {}

        def wload(w, r0, nk, c0, ncol, reuse=K_SCR):
            s = slot_rr[0] % NSLOT
            slot_rr[0] += 1
            key = (w.name, r0, nk, c0, ncol)
            dst = Wt[s][:, 0:nk, 0:ncol]
            if reuse and key in scratch:
                scr, skey_ = scratch[key]
                S.add("sp", R("dma_start", out=dst, in_=scr.rearrange("p (k c) -> p k c", c=ncol)), (skey_,), (f"W{s}",), dma_key=f"wslot{s}")
                return s
            src = w[r0:r0 + nk * 128, c0:c0 + ncol].rearrange("(k p) c -> p k c", p=128)
            S.add("pool", R("dma_start", out=dst, in_=src), (), (f"W{s}",), dma_key=f"wslot{s}")
            if reuse:
                nm = "scr%d" % len(scratch)
                scr = nc.dram_tensor(nm, [128, nk * ncol], BF16, kind="Internal").ap()
                scratch[key] = (scr, nm)
                S.add("sp", R("dma_start", out=scr.rearrange("p (k c) -> p k c", c=ncol), in_=dst), (f"W{s}",), (nm,), dma_key=f"scrst{s}")
            return s

        LD(R("dma_start", out=small[:], in_=small_d), "ld_small", ("small", "smallAB"))
        LD(R("dma_start", out=lgt[:], in_=rows_d[:, R_LG:R_LG + 8].partition_broadcast(128)), "ld_rows", ("lgt",))
        LD(R("dma_start", out=tmpf[1][:], in_=rows_d[:, R_BSP:R_BSP + 512].partition_broadcast(128)), "ld_rows4", ("tmpf1",))
        S.add("pool", R("dma_start", out=wsp_n[:], in_=w_sp.rearrange("g i j -> i g j")), (), ("wsp_n",), dma_key="ld_wsp")

        V(R("tensor_copy", out=identB[:], in_=c(O_ID, 128)), ("small",), ("identB",))
        V(R("memset", onesB[:], 1.0), (), ("onesB",))
        V(R("memset", cc[:], 0.0), (), ("cc",))
        A(R("activation", out=lg[:], in_=lgt[:], func=AF.Exp, scale=-1.0), ("lgt",), ("lg",))
        A(R("activation", out=lg[:], in_=lg[:], func=AF.Ln, bias=c(O_ONE), scale=1.0), ("lg", "small"), ("lg",))
        V(R("tensor_scalar", out=lg[:], in0=lg[:], scalar1=-1.0, scalar2=0.0, op0=ALU.mult, op1=ALU.add), ("lg",), ("lg",))
        V(R("tensor_copy", out=cc[:, 0:1], in_=c(O_ONE)), ("small", "cc"), ("cc",))
        V(R("tensor_copy", out=cc[:, 1:2], in_=c(O_FLAG)), ("small", "cc"), ("cc",))
        A(R("activation", out=cc[:, 8:16], in_=lg[:], func=AF.Exp, scale=128.0), ("lg", "cc"), ("cc",))
        V(R("tensor_scalar", out=cc[:, 16:24], in0=cc[:, 8:16], scalar1=c(O_FLAG), scalar2=0.0, op0=ALU.mult, op1=ALU.add), ("cc", "small"), ("cc",))
        for h in range(4):
            hs = slice(h * 128, (h + 1) * 128)
            V(R("tensor_scalar", out=tmpf[0][:, 0:128], in0=c(O_A, 128), scalar1=lg[:, h:h + 1], scalar2=0.0, op0=ALU.mult, op1=ALU.add), ("lg", "small", "smallAB"), ("tmpf0",))
            V(R("scalar_tensor_tensor", out=tmpf[0][:, 128:256], in0=c(O_B, 128), scalar=lg[:, 4 + h:5 + h], in1=tmpf[0][:, 0:128], op0=ALU.mult, op1=ALU.add), ("lg", "small", "smallAB", "tmpf0"), ("tmpf0",))
            A(R("activation", out=tmpf[0][:, 256:384], in_=tmpf[0][:, 128:256], func=AF.Exp), ("tmpf0",), ("tmpf0",))
            V(R("tensor_tensor", out=MT[:, hs], in0=tmpf[0][:, 256:384], in1=c(O_ID, 128), op=ALU.add), ("tmpf0", "small"), ("MT",))
            A(R("activation", out=kdf[:, hs], in_=c(O_R127, 128), func=AF.Exp, scale=lg[:, h:h + 1]), ("lg", "small", "smallAB"), ("kdf",))
            A(R("activation", out=kdb[:, hs], in_=c(O_RP, 128), func=AF.Exp, scale=lg[:, 4 + h:5 + h]), ("lg", "small", "smallAB"), ("kdb",))
            A(R("activation", out=dcol[:, h:h + 1], in_=c(O_RP1), func=AF.Exp, scale=lg[:, h:h + 1]), ("lg", "small"), ("dcol",))
            A(R("activation", out=dcol[:, 4 + h:5 + h], in_=c(O_R128), func=AF.Exp, scale=lg[:, 4 + h:5 + h]), ("lg", "small"), ("dcol",))
            V(R("tensor_scalar", out=Dq[:, h, :], in0=c(O_ID, 128), scalar1=dcol[:, h:h + 1], scalar2=0.0, op0=ALU.mult, op1=ALU.add), ("dcol", "small"), ("Dq",))
            V(R("tensor_scalar", out=Dq[:, 4 + h, :], in0=c(O_ID, 128), scalar1=dcol[:, 4 + h:5 + h], scalar2=0.0, op0=ALU.mult, op1=ALU.add), ("dcol", "small"), ("Dq",))
        PSM = pb()
        for g in range(4):
            T(R("matmul", ps[:, PSM, g * 128:(g + 1) * 128], lhsT=wsp_n[:, g, :], rhs=identB[:], start=True, stop=True), ("wsp_n", "identB"), (f"ps{PSM}",))
        A(R("activation", out=wspT[:].rearrange("p g i -> p (g i)"), in_=ps[:, PSM, :], func=AF.Copy), (f"ps{PSM}",), ("wspT",))
        PSR = pb()
        for g in range(4):
            T(R("matmul", ps[:, PSR, g * 128:(g + 1) * 128], lhsT=onesB[:], rhs=wspT[:, g, :], start=True, stop=True), ("onesB", "wspT"), (f"ps{PSR}",))
        for e8 in range(8):
            g = e8 // 2
            V(R("scalar_tensor_tensor", out=Bt[:, e8, :], in0=ps[:, PSR, g * 128:(g + 1) * 128], scalar=c(O_BSG + e8), in1=tmpf[1][:, g * 128:(g + 1) * 128], op0=ALU.mult, op1=ALU.add),
              (f"ps{PSR}", "small", "tmpf1"), ("Bt",))

        def load_x(c12, extra_rd=()):
            st = xs[c12 % 4]
            key = f"xs{c12 % 4}"
            LD(R("dma_start", out=st, in_=x_in[c12 * 128:(c12 + 1) * 128, :]), f"ld_x{c12 % 4}", (key,), rd=extra_rd)
            t = c12 // 4
            tok = slice(c12 * 128, (c12 + 1) * 128)
            for half in range(2):
                b = pb()
                for j in range(4):
                    k = half * 4 + j
                    T(R("transpose", out=ps[:, b, j * 128:(j + 1) * 128], in_=st[:, k * 128:(k + 1) * 128], identity=c(O_ID, 128)),
                      (key, "small"), (f"ps{b}",))
                wr = tuple(f"x{half * 4 + j}_{t}" for j in range(4))
                eng = A if half == 0 else V
                if half == 0:
                    A(R("activation", out=x_all[:, half * 4:half * 4 + 4, tok], in_=ps[:, b, :].rearrange("p (a q) -> p a q", q=128), func=AF.Copy), (f"ps{b}",), wr)
                else:
                    V(R("tensor_copy", out=x_all[:, half * 4:half * 4 + 4, tok], in_=ps[:, b, :].rearrange("p (a q) -> p a q", q=128)), (f"ps{b}",), wr)

        A(R("activation", out=cT[:].rearrange("p k s -> p (k s)"), in_=c(O_C2, 16), func=AF.Silu), ("small",), ("cT",))
        modv = mod[:].rearrange("p (i k) s -> p i k s", k=8)
        gnv = c(O_GN, 48).rearrange("p (n k) -> p n k", k=8)

        modq = list(range(18))

        def mod_block():
            if not modq:
                return
            blk = modq.pop(0)
            L, i_ = blk // 6, (blk % 6) // 2
            tag = S.tag
            S.tag = f"mod{L}"
            s_ = wload(w_ada, 0, 8, blk * 512, 512, reuse=False)
            PSM = pb()
            for m in range(4):
                for k in range(8):
                    T(R("matmul", ps[:, PSM, m * 2:m * 2 + 2], lhsT=Wt[s_][:, k, m * 128:(m + 1) * 128], rhs=cT[:, k, :], start=(k == 0), stop=(k == 7)),
                      (f"W{s_}", "cT"), (f"ps{PSM}",))
            j0 = blk * 4
            V(R("tensor_tensor", out=mod[:, j0:j0 + 4, :], in0=ps[:, PSM, 0:8].rearrange("p (j s) -> p j s", s=2),
                in1=c(O_BADA + j0, 4).unsqueeze(2).to_broadcast([128, 4, 2]), op=ALU.add), (f"ps{PSM}", "small"), (f"mod{L}_{i_}",))
            if blk % 2 == 1:
                coef = 1.0 if L == 1 else 0.5
                if i_ == 1:
                    V(R("scalar_tensor_tensor", out=Apre[:, L], in0=modv[:, 3 * L + 1], scalar=1.0, in1=gnv[:, 2 * L].unsqueeze(2).to_broadcast([128, 8, 2]), op0=ALU.add, op1=ALU.mult),
                      (f"mod{L}_1", "small"), (f"Apre{L}",))
                if i_ == 2:
                    V(R("scalar_tensor_tensor", out=Gpost[:, L], in0=modv[:, 3 * L + 2], scalar=coef, in1=gnv[:, 2 * L + 1].unsqueeze(2).to_broadcast([128, 8, 2]), op0=ALU.mult, op1=ALU.mult),
                      (f"mod{L}_2", "small"), (f"Gpost{L}",))
            S.tag = tag

        for c12 in range(4):
            load_x(c12)
        for i_ in range(6):
            mod_block()
            if i_ == 3:
                for c12 in range(4, 12):
                    load_x(c12, extra_rd=(f"W{(slot_rr[0] - 1) % NSLOT}",))

        def xk(k, t):
            return f"x{k}_{t}"

        def rstd_from_psn(n_inv):
            A(R("activation", out=rstd_t[:], in_=ps[:, PSN, :], func=AF.Sqrt, bias=c(O_EPS), scale=n_inv), ("ps7", "small"), ("rstd_t",))
            V(R("reciprocal", out=rstd_t[:], in_=rstd_t[:]), ("rstd_t",), ("rstd_t",))

        rstd_p = small[:, O_A:O_A + 512]
        NSEQ = [(0, 0), (0, 1), (0, 2), (1, 1), (1, 0), (1, 1), (1, 2), (2, 0), (2, 1), (2, 2)]
        nptr = {"made": 0, "used": 0, "res": {}}

        def _norm_pre_emit(L, t):
            cur["i"] ^= 1
            hT, hp = hTs[cur["i"]], "hT%d_" % (0 if K_HT1 else cur["i"])
            ts = slice(t * 512, (t + 1) * 512)
            sl = 0 if t < 2 else 1
            bn = pb()
            for k in range(8):
                q = sqb[k % 2]
                A(R("activation", out=q[:], in_=x_all[:, k, ts], func=AF.Square), (xk(k, t),), (f"sqb{k % 2}",))
                T(R("matmul", ps[:, bn, :], lhsT=onesB[:], rhs=q[:], start=(k == 0), stop=(k == 7)), (f"sqb{k % 2}", "onesB"), (f"ps{bn}",))
            A(R("activation", out=rstd_p, in_=ps[:, bn, :], func=AF.Sqrt, bias=c(O_EPS), scale=1.0 / D), (f"ps{bn}", "small"), ("smallAB",))
            V(R("reciprocal", out=rstd_p, in_=rstd_p), ("smallAB",), ("smallAB",))
            for k in range(8):
                tf = tmpf[k % 3]
                V(R("scalar_tensor_tensor", out=tf[:], in0=x_all[:, k, ts], scalar=Apre[:, L, k, sl:sl + 1], in1=rstd_p, op0=ALU.mult, op1=ALU.mult),
                  (xk(k, t), f"Apre{L}", "smallAB"), (f"tmpf{k % 3}",))
                A(R("activation", out=hT[:, k, :], in_=tf[:], func=AF.Identity, bias=modv[:, 3 * L, k, sl:sl + 1], scale=1.0),
                  (f"tmpf{k % 3}", f"mod{L}_0"), (hp + str(k),))
            return hT, hp

        def prefetch_norm():
            i = nptr["made"]
            if i < len(NSEQ) and i == nptr["used"] and K_PREN:
                tag = S.tag
                nptr["res"][i] = _norm_pre_emit(*NSEQ[i])
                nptr["made"] = i + 1
                S.tag = tag

        def norm_pre(L, t):
            i = nptr["used"]
            assert NSEQ[i] == (L, t), (NSEQ[i], L, t)
            if nptr["made"] == i:
                nptr["res"][i] = _norm_pre_emit(L, t)
                nptr["made"] = i + 1
            nptr["used"] = i + 1
            return nptr["res"].pop(i)


        def post_norm(L, t):
            ts = slice(t * 512, (t + 1) * 512)
            sl = 0 if t < 2 else 1
            rstd_from_psn(1.0 / D)
            for k in range(8):
                tf = tmpf[k % 3]
                V(R("scalar_tensor_tensor", out=tf[:], in0=y_sb[:, k, :], scalar=Gpost[:, L, k, sl:sl + 1], in1=rstd_t[:], op0=ALU.mult, op1=ALU.mult),
                  (f"ysb{k}", f"Gpost{L}", "rstd_t"), (f"tmpf{k % 3}",))
                V(R("tensor_tensor", out=x_all[:, k, ts], in0=x_all[:, k, ts], in1=tf[:], op=ALU.add),
                  (f"tmpf{k % 3}", xk(k, t)), (xk(k, t),))

        def out_proj_rows(wd, nkc, rhs_fn, rhs_keys):
            kblocks = [(i, min(8, nkc - i)) for i in range(0, nkc, 8)]
            first_sq = [True]
            for cb in range(2):
                banks = [pb() for _ in range(4)]
                for (k0, nk) in kblocks:
                    s = wload(wd, k0 * 128, nk, cb * 512, 512)
                    for m in range(4):
                        for k in range(nk):
                            kk = k0 + k
                            T(R("matmul", ps[:, banks[m], :], lhsT=Wt[s][:, k, m * 128:(m + 1) * 128], rhs=rhs_fn(kk), start=(kk == 0), stop=(kk == nkc - 1)),
                              (f"W{s}",) + rhs_keys(kk), (f"ps{banks[m]}",))
                for m in range(4):
                    e_ = cb * 4 + m
                    b = banks[m]
                    q = sqb[e_ % 2]
                    A(R("activation", out=y_sb[:, e_, :], in_=ps[:, b, :], func=AF.Copy), (f"ps{b}",), (f"ysb{e_}",))
                    A(R("activation", out=q[:], in_=ps[:, b, :], func=AF.Square), (f"ps{b}",), (f"sqb{e_ % 2}",))
                    T(R("matmul", ps[:, PSN, :], lhsT=onesB[:], rhs=q[:], start=(e_ == 0), stop=(e_ == 7)), (f"sqb{e_ % 2}", "onesB"), ("ps7",))

        def ffn(L, t):
            fi = 0 if L == 0 else 1
            S.tag = f"ffn{L}_t{t}"
            hT, hp = norm_pre(L, t)
            for cb in range(6):
                ncol = 512 if cb < 5 else 256
                sa = wload(w1[fi], 0, 8, cb * 512, ncol)
                sb_ = wload(w1[fi], 0, 8, DFF + cb * 512, ncol)
                for m in range(ncol // 128):
                    j = cb * 4 + m
                    ba, bb = pb(), pb()
                    for k in range(8):
                        T(R("matmul", ps[:, ba, :], lhsT=Wt[sa][:, k, m * 128:(m + 1) * 128], rhs=hT[:, k, :], start=(k == 0), stop=(k == 7)),
                          (f"W{sa}", hp + str(k)), (f"ps{ba}",))
                    for k in range(8):
                        T(R("matmul", ps[:, bb, :], lhsT=Wt[sb_][:, k, m * 128:(m + 1) * 128], rhs=hT[:, k, :], start=(k == 0), stop=(k == 7)),
                          (f"W{sb_}", hp + str(k)), (f"ps{bb}",))
                    tf = tmpf[j % 3]
                    A(R("activation", out=tf[:], in_=ps[:, ba, :], func=AF.Silu), (f"ps{ba}",), (f"tmpf{j % 3}",))
                    V(R("tensor_tensor", out=hidden[:, j, :], in0=tf[:], in1=ps[:, bb, :], op=ALU.mult), (f"ps{bb}", f"tmpf{j % 3}"), (f"hid{j}",))
                if L == 0 and cb in MODPTS and len(modq) > 6:
                    mod_block()
            prefetch_norm()
            out_proj_rows(w2[fi], 22, lambda kk: hidden[:, kk, :], lambda kk: (f"hid{kk}",))
            post_norm(L, t)

        def rope_ops(src, dst, cidx, rd_key, wr_key):
            sv = src.rearrange("p (h two d) -> p h two d", h=4, two=2)
            dv = dst.rearrange("p (h two d) -> p h two d", h=4, two=2)
            x1, x2 = sv[:, :, 0, :], sv[:, :, 1, :]
            cs = cs_t[:, cidx:cidx + 1, :].to_broadcast([128, 4, 64])
            sn = sn_t[:, cidx:cidx + 1, :].to_broadcast([128, 4, 64])
            V(R("tensor_tensor", out=rtmp[:, 0], in0=x1, in1=cs, op=ALU.mult), (rd_key, "rope"), ("rtmp0",))
            V(R("tensor_tensor", out=rtmp[:, 1], in0=x2, in1=sn, op=ALU.mult), (rd_key, "rope"), ("rtmp1",))
            V(R("tensor_tensor", out=dv[:, :, 0, :], in0=rtmp[:, 0], in1=rtmp[:, 1], op=ALU.subtract), ("rtmp0", "rtmp1"), (wr_key,))
            V(R("tensor_tensor", out=rtmp[:, 2], in0=x1, in1=sn, op=ALU.mult), (rd_key, "rope"), ("rtmp2",))
            V(R("tensor_tensor", out=rtmp[:, 3], in0=x2, in1=cs, op=ALU.mult), (rd_key, "rope"), ("rtmp3",))
            V(R("tensor_tensor", out=dv[:, :, 1, :], in0=rtmp[:, 2], in1=rtmp[:, 3], op=ALU.add), ("rtmp2", "rtmp3"), (wr_key,))

        def stats_small(o_sum, o_ssq, o_out, ncol, n, sqrt_rd=()):
            sm = stat[:, o_sum:o_sum + ncol]
            sq_ = stat[:, o_ssq:o_ssq + ncol]
            mean = stat[:, 32:32 + ncol]
            m2 = stat[:, 36:36 + ncol]
            var = stat[:, 40:40 + ncol]
            sd = stat[:, 44:44 + ncol]
            rstd = stat[:, o_out:o_out + ncol]
            nmr = stat[:, o_out + ncol:o_out + 2 * ncol]
            V(R("tensor_scalar", out=mean, in0=sm, scalar1=1.0 / n, scalar2=0.0, op0=ALU.mult, op1=ALU.add), ("stat",), ("stat",))
            V(R("tensor_tensor", out=m2, in0=mean, in1=mean, op=ALU.mult), ("stat",), ("stat",))
            V(R("scalar_tensor_tensor", out=var, in0=sq_, scalar=1.0 / n, in1=m2, op0=ALU.mult, op1=ALU.subtract), ("stat",), ("stat",))
            V(R("tensor_scalar", out=var, in0=var, scalar1=0.0, scalar2=0.0, op0=ALU.max, op1=ALU.add), ("stat",), ("stat",))
            A(R("activation", out=sd, in_=var, func=AF.Sqrt, bias=c(O_EPS), scale=1.0), ("stat", "small") + tuple(sqrt_rd), ("stat",))
            V(R("reciprocal", out=rstd, in_=sd), ("stat",), ("stat",))
            V(R("scalar_tensor_tensor", out=nmr, in0=mean, scalar=-1.0, in1=rstd, op0=ALU.mult, op1=ALU.mult), ("stat",), ("stat",))

        saft_i = {"f": 0, "b": 0}

        def carry_type(t, cfrom, cto):
            lo = min(cfrom, cto)
            if lo == 1:
                return 1 if t < 2 else 2
            return 0

        def state_step(dirn, t, cidx, prev_ap, prev_key, ct, psb, out_pair):
            d = 0 if dirn == "f" else 1
            bufs = SaftF if dirn == "f" else SaftB
            ni = saft_i[dirn]
            saft_i[dirn] = 1 - ni
            newb = bufs[ni]
            nkey = skey(dirn, ni)
            if dirn == "f":
                V(R("tensor_scalar", out=Sfbf[:], in0=prev_ap, scalar1=cc[:, ct:ct + 1], scalar2=0.0, op0=ALU.mult, op1=ALU.add), (prev_key, "cc"), ("Sfbf",))
            else:
                A(R("activation", out=Sbbf[:, cidx, :], in_=prev_ap, func=AF.Identity, scale=cc[:, ct:ct + 1]), (prev_key, "cc"), (f"Sbbf{cidx}",))
            for h in range(4):
                col = 8 + ct * 8 + d * 4 + h
                bank = psb + h // 2
                V(R("scalar_tensor_tensor", out=newb[:, h * 256:(h + 1) * 256], in0=prev_ap[:, h * 256:(h + 1) * 256], scalar=cc[:, col:col + 1],
                                                                          in1=ps[:, bank, (h % 2) * 256:(h % 2) * 256 + 256], op0=ALU.mult, op1=ALU.add),
                  (prev_key, "cc", f"ps{bank}"), (nkey,))
            if out_pair is not None:
                S.out_dmas.append(LD(R("dma_start", out=st_out[t, out_pair, d], in_=newb[:]), f"st_{dirn}{ni}", (), rd=(nkey,)))
            return newb[:], nkey

        state = {}

        def mixer_M12(t, full=True):
            S.tag = f"M12_t{t}" + ("" if full else "pre")
            hT, hp = norm_pre(1, t)
            state["hT"] = (hT, hp)
            LD(R("dma_start", out=cs_t[:], in_=rope_d[t, 0]), "ld_rope", ("rope",))
            LD(R("dma_start", out=sn_t[:], in_=rope_d[t, 1]), "ld_rope", ("rope",))
            sk = wload(w_in, 0, 8, 512, 512)
            sv0 = wload(w_in, 0, 8, 1024, 512)
            sv1 = wload(w_in, 0, 8, 1536, 512)
            pi_ = 1 - saft_i["b"]
            if t == 1:
                LD(R("dma_start", out=SaftB[pi_][:], in_=s0_d[1]), "ld_s0b", (skey("b", pi_),))
                prev, pkey, ct = SaftB[pi_][:], skey("b", pi_), 0
            elif t == 0:
                prev, pkey, ct = state["bX"][0], state["bX"][1], 1
            else:
                prev, pkey, ct = SaftB[pi_][:], skey("b", pi_), 2
            for cidx in (3, 2, 1, 0):
                tk = slice(cidx * 128, (cidx + 1) * 128)
                bk = pb()
                bv = pb2()
                for k in range(8):
                    T(R("matmul", ps[:, bk, :], lhsT=hT[:, k, tk], rhs=Wt[sk][:, k, :], start=(k == 0), stop=(k == 7)), (f"W{sk}", hp + str(k)), (f"ps{bk}",))
                for k in range(8):
                    T(R("matmul", ps[:, bv, :], lhsT=hT[:, k, tk], rhs=Wt[sv0][:, k, :], start=(k == 0), stop=(k == 7)), (f"W{sv0}", hp + str(k)), (f"ps{bv}",))
                for k in range(8):
                    T(R("matmul", ps[:, bv + 1, :], lhsT=hT[:, k, tk], rhs=Wt[sv1][:, k, :], start=(k == 0), stop=(k == 7)), (f"W{sv1}", hp + str(k)), (f"ps{bv + 1}",))
                A(R("activation", out=kf32, in_=ps[:, bk, :], func=AF.Copy), (f"ps{bk}",), ("kf32",))
                A(R("activation", out=v_tm[:, cidx, :].rearrange("p (a q) -> p a q", q=512), in_=ps[:, bv:bv + 2, :], func=AF.Copy), (f"ps{bv}", f"ps{bv + 1}"), (f"v{cidx}",))
                rope_ops(kf32, krot, cidx, "kf32", "krot")
                if full:
                    V(R("tensor_tensor", out=kfw[:, cidx, :], in0=krot, in1=kdf[:], op=ALU.mult), ("krot", "kdf"), (f"kfw{cidx}",))
                    bt = pb()
                    for h in range(4):
                        T(R("matmul", ps[:, bt, h * 128:(h + 1) * 128], lhsT=krot[:, h * 128:(h + 1) * 128], rhs=identB[:], start=True, stop=True), ("krot", "identB"), (f"ps{bt}",))
                    A(R("activation", out=kT[:, cidx, :], in_=ps[:, bt, :], func=AF.Copy), (f"ps{bt}",), (f"kT{cidx}",))
                V(R("tensor_tensor", out=kbw, in0=krot, in1=kdb[:], op=ALU.mult), ("krot", "kdb"), ("qT2",))
                bs_ = pb2()
                for h in range(4):
                    T(R("matmul", ps[:, bs_ + h // 2, (h % 2) * 256:(h % 2) * 256 + 256], lhsT=kbw[:, h * 128:(h + 1) * 128], rhs=v_tm[:, cidx, h * 256:(h + 1) * 256], start=True, stop=True),
                      ("qT2", f"v{cidx}"), (f"ps{bs_ + h // 2}",))
                prev, pkey = state_step("b", t, cidx, prev, pkey, ct, bs_, (cidx // 2) if (full and cidx in (0, 2)) else None)
                if cidx > 0:
                    ct = carry_type(t, cidx, cidx - 1)
            state["bX"] = (prev, pkey)
            if not full:
                prefetch_norm()

        def mixer_rest(t):
            ts = slice(t * 512, (t + 1) * 512)
            hT, hp = state["hT"]
            S.tag = f"M3_t{t}"
            sq_ = wload(w_in, 0, 8, 0, 512)
            sg0 = wload(w_in, 0, 8, 2048, 512)
            sg1 = wload(w_in, 0, 8, 2560, 512)
            pi_ = 1 - saft_i["f"]
            if t == 0:
                LD(R("dma_start", out=SaftF[pi_][:], in_=s0_d[0]), "ld_s0f", (skey("f", pi_),))
                prev, pkey, ct = SaftF[pi_][:], skey("f", pi_), 0
            elif t == 1:
                prev, pkey, ct = state["fX"][0], state["fX"][1], 1
            else:
                prev, pkey, ct = SaftF[pi_][:], skey("f", pi_), 2
            for cidx in range(4):
                tk = slice(cidx * 128, (cidx + 1) * 128)
                bq = pb()
                bg = pb2()
                for k in range(8):
                    T(R("matmul", ps[:, bq, :], lhsT=hT[:, k, tk], rhs=Wt[sq_][:, k, :], start=(k == 0), stop=(k == 7)), (f"W{sq_}", hp + str(k)), (f"ps{bq}",))
                for k in range(8):
                    T(R("matmul", ps[:, bg, :], lhsT=hT[:, k, tk], rhs=Wt[sg0][:, k, :], start=(k == 0), stop=(k == 7)), (f"W{sg0}", hp + str(k)), (f"ps{bg}",))
                for k in range(8):
                    T(R("matmul", ps[:, bg + 1, :], lhsT=hT[:, k, tk], rhs=Wt[sg1][:, k, :], start=(k == 0), stop=(k == 7)), (f"W{sg1}", hp + str(k)), (f"ps{bg + 1}",))
                A(R("activation", out=qf32, in_=ps[:, bq, :], func=AF.Identity, scale=128.0 ** -0.5), (f"ps{bq}",), ("qf32",))
                A(R("activation", out=sgl.rearrange("p (a q) -> p a q", q=512), in_=ps[:, bg:bg + 2, :], func=AF.Silu), (f"ps{bg}", f"ps{bg + 1}"), ("sgl",))
                rope_ops(qf32, qrot, cidx, "qf32", "qrot")
                b3 = [pb(), pb(), pb()]
                for h in range(4):
                    hs = slice(h * 128, (h + 1) * 128)
                    T(R("matmul", ps[:, b3[0], hs], lhsT=qrot[:, hs], rhs=identB[:], start=True, stop=True), ("qrot", "identB"), (f"ps{b3[0]}",))
                    T(R("matmul", ps[:, b3[1], hs], lhsT=qrot[:, hs], rhs=Dq[:, h, :], start=True, stop=True), ("qrot", "Dq"), (f"ps{b3[1]}",))
                    T(R("matmul", ps[:, b3[2], hs], lhsT=qrot[:, hs], rhs=Dq[:, 4 + h, :], start=True, stop=True), ("qrot", "Dq"), (f"ps{b3[2]}",))
                A(R("activation", out=qT3[:, 0, :], in_=ps[:, b3[0], :], func=AF.Copy), (f"ps{b3[0]}",), ("qT0",))
                V(R("tensor_copy", out=qT3[:, 1, :], in_=ps[:, b3[1], :]), (f"ps{b3[1]}",), ("qT1",))
                A(R("activation", out=qT3[:, 2, :], in_=ps[:, b3[2], :], func=AF.Copy), (f"ps{b3[2]}",), ("qT2",))
                if K_TBLPF:
                    A(R("activation", out=stat[:, 24:25], in_=c(O_ONE), func=AF.Sqrt), ("qT2", "sgl", "small"), ("statpf",))
                bp = pb()
                for h in range(4):
                    hs = slice(h * 128, (h + 1) * 128)
                    T(R("matmul", ps[:, bp, hs], lhsT=kT[:, cidx, hs], rhs=qT3[:, 0, hs], start=True, stop=True), (f"kT{cidx}", "qT0"), (f"ps{bp}",))
                V(R("tensor_tensor", out=Pm, in0=ps[:, bp, :], in1=MT[:], op=ALU.mult), (f"ps{bp}", "MT"), ("Pm",))
                bsf = pb2()
                for h in range(4):
                    T(R("matmul", ps[:, bsf + h // 2, (h % 2) * 256:(h % 2) * 256 + 256], lhsT=kfw[:, cidx, h * 128:(h + 1) * 128], rhs=v_tm[:, cidx, h * 256:(h + 1) * 256], start=True, stop=True),
                      (f"kfw{cidx}", f"v{cidx}"), (f"ps{bsf + h // 2}",))
                prev, pkey = state_step("f", t, cidx, prev, pkey, ct, bsf, (cidx // 2) if cidx in (1, 3) else None)
                if cidx < 3:
                    ct = carry_type(t, cidx, cidx + 1)
                bo = pb2()
                for h in range(4):
                    hs = slice(h * 128, (h + 1) * 128)
                    es_ = slice(h * 256, (h + 1) * 256)
                    oo = ps[:, bo + h // 2, (h % 2) * 256:(h % 2) * 256 + 256]
                    T(R("matmul", oo, lhsT=Pm[:, hs], rhs=v_tm[:, cidx, es_], start=True, stop=False), ("Pm", f"v{cidx}"), (f"ps{bo + h // 2}",))
                    T(R("matmul", oo, lhsT=qT3[:, 1, hs], rhs=Sfbf[:, es_], start=False, stop=False), ("qT1", "Sfbf"), (f"ps{bo + h // 2}",))
                    T(R("matmul", oo, lhsT=qT3[:, 2, hs], rhs=Sbbf[:, cidx, es_], start=False, stop=True), ("qT2", f"Sbbf{cidx}"), (f"ps{bo + h // 2}",))
                for h in range(4):
                    oo = ps[:, bo + h // 2, (h % 2) * 256:(h % 2) * 256 + 256]
                    A(R("activation", out=r_t[:, h * 256:(h + 1) * 256], in_=oo, func=AF.Copy, accum_out=stat[:, h:h + 1]), (f"ps{bo + h // 2}",), (f"r_t{h}", "stat"))
                    A(R("activation", out=rg[:, h * 256:(h + 1) * 256], in_=oo, func=AF.Square, accum_out=stat[:, 4 + h:5 + h]), (f"ps{bo + h // 2}",), ("rg", "stat"))
                stats_small(0, 4, 8, 4, 256.0)
                for h in range(4):
                    A(R("activation", out=r_t[:, h * 256:(h + 1) * 256], in_=r_t[:, h * 256:(h + 1) * 256], func=AF.Identity, scale=stat[:, 8 + h:9 + h], bias=stat[:, 12 + h:13 + h]),
                      (f"r_t{h}", "stat"), (f"r_t{h}",))
                V(R("tensor_tensor", out=rg[:], in0=r_t, in1=sgl, op=ALU.mult), ("r_t", "sgl"), ("rg",))
                bt = pb2()
                for e8 in range(8):
                    T(R("matmul", ps[:, bt + e8 // 4, (e8 % 4) * 128:(e8 % 4) * 128 + 128], lhsT=rg[:, e8 * 128:(e8 + 1) * 128], rhs=identB[:], start=True, stop=True),
                      ("rg", "identB"), (f"ps{bt + e8 // 4}",))
                for half in range(2):
                    V(R("tensor_tensor", out=rgT[:, half * 4:half * 4 + 4, tk], in0=ps[:, bt + half, :].rearrange("p (a q) -> p a q", q=128),
                                                           in1=c(O_GRET + half * 4, 4).unsqueeze(2).to_broadcast([128, 4, 128]), op=ALU.mult),
                      (f"ps{bt + half}", "small"), (f"rgT{cidx}",))
            state["fX"] = (prev, pkey)
            if len(modq) > 0 and K_MODSPREAD:
                mod_block()
                mod_block()
            S.tag = f"M4_t{t}"
            su0 = wload(w_in, 0, 8, 3072, 512)
            su1 = wload(w_in, 0, 8, 3584, 512)
            for blk, s in ((0, su0), (1, su1)):
                for m in range(4):
                    e_ = blk * 4 + m
                    b = pb()
                    for k in range(8):
                        T(R("matmul", ps[:, b, :], lhsT=Wt[s][:, k, m * 128:(m + 1) * 128], rhs=hT[:, k, :], start=(k == 0), stop=(k == 7)), (f"W{s}", hp + str(k)), (f"ps{b}",))
                    A(R("activation", out=u_act[:, e_, :], in_=ps[:, b, :], func=AF.Gelu_apprx_tanh), (f"ps{b}",), (f"ysb{e_}",))
            sv0 = wload(w_in, 0, 8, 4096, 512)
            sv1 = wload(w_in, 0, 8, 4608, 512)
            for pr in range(2):
                info = {}
                for cidx in (2 * pr, 2 * pr + 1):
                    tk = slice(cidx * 128, (cidx + 1) * 128)
                    par = cidx % 2
                    VS_, vsn_ = (VS, VS2)[par], (vsn, vsn2)[par]
                    KVS, Kvsn = ("VS", "VS2")[par], ("vsn", "vsn2")[par]
                    so = 16 + 4 * par
                    bv = pb2()
                    for k in range(8):
                        T(R("matmul", ps[:, bv, :], lhsT=hT[:, k, tk], rhs=Wt[sv0][:, k, :], start=(k == 0), stop=(k == 7)), (f"W{sv0}", hp + str(k)), (f"ps{bv}",))
                    for k in range(8):
                        T(R("matmul", ps[:, bv + 1, :], lhsT=hT[:, k, tk], rhs=Wt[sv1][:, k, :], start=(k == 0), stop=(k == 7)), (f"W{sv1}", hp + str(k)), (f"ps{bv + 1}",))
                    A(R("activation", out=VS_.rearrange("p (a q) -> p a q", q=512), in_=ps[:, bv:bv + 2, :], func=AF.Gelu_apprx_tanh, accum_out=stat[:, so:so + 1]), (f"ps{bv}", f"ps{bv + 1}"), (KVS, "stat"))
                    A(R("activation", out=vsn_, in_=VS_, func=AF.Square, accum_out=stat[:, so + 1:so + 2]), (KVS,), (Kvsn, "stat"))
                    info[cidx] = (tk, par, VS_, vsn_, KVS, Kvsn, so)
                for cidx in (2 * pr, 2 * pr + 1):
                    tk, par, VS_, vsn_, KVS, Kvsn, so = info[cidx]
                    other = info[2 * pr + 1][5] if cidx == 2 * pr else ()
                    stats_small(so, so + 1, so + 2, 1, 1024.0, sqrt_rd=((other,) if other else ()))
                for cidx in (2 * pr, 2 * pr + 1):
                    tk, par, VS_, vsn_, KVS, Kvsn, so = info[cidx]
                    A(R("activation", out=vsn_, in_=VS_, func=AF.Identity, scale=stat[:, so + 2:so + 3], bias=stat[:, so + 3:so + 4]), (KVS, "stat"), (Kvsn,))
                    bsp = pb2()
                    for e8 in range(8):
                        T(R("matmul", ps[:, bsp + e8 // 4, (e8 % 4) * 128:(e8 % 4) * 128 + 128], lhsT=vsn_[:, e8 * 128:(e8 + 1) * 128], rhs=wspT[:, e8 // 2, :], start=True, stop=True),
                          (Kvsn, "wspT"), (f"ps{bsp + e8 // 4}",))
                    for e8 in range(8):
                        V(R("scalar_tensor_tensor", out=VS_[:, e8 * 128:(e8 + 1) * 128], in0=ps[:, bsp + e8 // 4, (e8 % 4) * 128:(e8 % 4) * 128 + 128], scalar=c(O_GSG + e8), in1=Bt[:, e8, :], op0=ALU.mult, op1=ALU.add),
                          (f"ps{bsp + e8 // 4}", "small", "Bt"), ((f"UB{18 + e8 // 2}",) if par == 0 else (f"VS2_{e8 // 2}",)))
                    V(R("tensor_tensor", out=su[:, :, tk], in0=VS_.rearrange("p (e i) -> p e i", i=128), in1=u_act[:, :, tk], op=ALU.mult),
                      (KVS,) + tuple(f"ysb{e}" for e in range(8)), tuple(f"sub{e}" for e in range(8)))
            if len(modq) > 0 and K_MODSPREAD:
                mod_block()
            S.tag = f"M5_t{t}"
            RGT = tuple(f"rgT{i}" for i in range(4))
            SU = tuple(f"su{i}" for i in range(4))
            for cb in range(2):
                sr = wload(w_ret, 0, 8, cb * 512, 512)
                ss = wload(w_sg, 0, 8, cb * 512, 512)
                sgr = wload(w_in, 0, 8, 5120 + cb * 512, 512)
                sgs = wload(w_in, 0, 8, 6144 + cb * 512, 512)
                for m in range(4):
                    e_ = cb * 4 + m
                    b1, b2, b3_, b4 = pb(), pb(), pb(), pb()
                    ms = slice(m * 128, (m + 1) * 128)
                    for k in range(8):
                        T(R("matmul", ps[:, b1, :], lhsT=Wt[sr][:, k, ms], rhs=rgT[:, k, :], start=(k == 0), stop=(k == 7)), (f"W{sr}",) + RGT, (f"ps{b1}",))
                    for k in range(8):
                        T(R("matmul", ps[:, b2, :], lhsT=Wt[ss][:, k, ms], rhs=su[:, k, :], start=(k == 0), stop=(k == 7)), (f"W{ss}", f"sub{k}"), (f"ps{b2}",))
                    for k in range(8):
                        T(R("matmul", ps[:, b3_, :], lhsT=Wt[sgr][:, k, ms], rhs=hT[:, k, :], start=(k == 0), stop=(k == 7)), (f"W{sgr}", hp + str(k)), (f"ps{b3_}",))
                    for k in range(8):
                        T(R("matmul", ps[:, b4, :], lhsT=Wt[sgs][:, k, ms], rhs=hT[:, k, :], start=(k == 0), stop=(k == 7)), (f"W{sgs}", hp + str(k)), (f"ps{b4}",))
                    A(R("activation", out=tmpf[0][:], in_=ps[:, b3_, :], func=AF.Sigmoid), (f"ps{b3_}",), ("tmpf0",))
                    A(R("activation", out=tmpf[1][:], in_=ps[:, b4, :], func=AF.Sigmoid), (f"ps{b4}",), ("tmpf1",))
                    V(R("tensor_tensor", out=tmpf[0][:], in0=tmpf[0][:], in1=ps[:, b1, :], op=ALU.mult), ("tmpf0", f"ps{b1}"), ("tmpf0",))
                    V(R("tensor_tensor", out=tmpf[1][:], in0=tmpf[1][:], in1=ps[:, b2, :], op=ALU.mult), ("tmpf1", f"ps{b2}"), ("tmpf1",))
                    V(R("tensor_tensor", out=ypre[:, e_, :], in0=tmpf[0][:], in1=tmpf[1][:], op=ALU.add), ("tmpf0", "tmpf1"), (f"ypre{e_}",))
            prefetch_norm()
            S.tag = f"M6_t{t}"
            out_proj_rows(w_o, 8, lambda kk: ypre[:, kk, :], lambda kk: (f"ypre{kk}",))
            post_norm(1, t)

        for t in range(3):
            ffn(0, t)
        while len(modq) > 6:
            mod_block()
        if not K_MODSPREAD:
            while modq:
                mod_block()
        mixer_M12(1, full=False)
        for t in range(3):
            mixer_M12(t, full=True)
            mixer_rest(t)
        def out_tile(t):
            S.tag = "out"
            UFv = UF[:, :].rearrange("p (c f) -> p c f", f=1024)
            allxs = tuple(f"xs{i}" for i in range(4))
            for k in range(8):
                b = pb()
                for c4 in range(4):
                    tok = slice((4 * t + c4) * 128, (4 * t + c4 + 1) * 128)
                    T(R("transpose", out=ps[:, b, c4 * 128:(c4 + 1) * 128], in_=x_all[:, k, tok], identity=c(O_ID, 128)), (xk(k, t), "small"), (f"ps{b}",))
                src = ps[:, b, :].rearrange("p (c q) -> p c q", q=128)
                if k % 2 == 0:
                    A(R("activation", out=UFv[:, :, k * 128:(k + 1) * 128], in_=src, func=AF.Copy), (f"ps{b}",), allxs)
                else:
                    V(R("tensor_copy", out=UFv[:, :, k * 128:(k + 1) * 128], in_=src), (f"ps{b}",), allxs)
            for c4 in range(4):
                c12 = 4 * t + c4
                S.out_dmas.append(LD(R("dma_start", out=y_out[c12 * 128:(c12 + 1) * 128, :], in_=xs[c4]), f"st_y{c4}", (), rd=(f"xs{c4}",)))

        while modq:
            mod_block()
        for t in range(3):
            ffn(2, t)
            out_tile(t)

        S.reorder({"pe": K_WIN[0], "act": K_WIN[1], "dve": K_WIN[2], "pool": 1, "sp": 4})
        S.resolve()
        _DBG["S"] = S
        sem_names = set()
        for e in S.ENGS:
            for op in S.ops[e]:
                if op.is_dma:
                    sem_names.add(op.sem)
        sems = {n: es.enter_context(nc.semaphore(n)) for n in sorted(sem_names)}
        esem = {e: es.enter_context(nc.semaphore("eng_" + e)) for e in ("pe", "act", "dve")}
        block = es.enter_context(nc.Block())

        def emit(eng_name):
            def body(e):
                for op in S.ops[eng_name]:
                    for (key, v) in op.waits:
                        e.wait_ge(sems[key[1]] if key[0] == "dma" else esem[key[1]], v)
                    ins = op.fn(e)
                    if op.is_dma:
                        ins.then_inc(sems[op.sem], 16)
                    elif op.inc:
                        ins.then_inc(esem[eng_name], 1)
                if eng_name == "sp":
                    final = {}
                    for op in S.out_dmas:
                        final[op.sem] = max(final.get(op.sem, 0), op.val)
                    for k, v in final.items():
                        e.wait_ge(sems[k], v)
            return body

        block.tensor(emit("pe"))
        block.scalar(emit("act"))
        block.vector(emit("dve"))
        block.gpsimd(emit("pool"))
        block.sync(emit("sp"))
    return nc


def _rope_tables(L):
    rows = L // 64
    r = np.repeat(np.arange(rows), 64).astype(np.float32)
    col = (np.arange(rows * 64) % 64).astype(np.float32)
    nf = 32
    freqs = (np.float32(10000.0) ** (-np.arange(nf, dtype=np.float32) / np.float32(nf))).astype(np.float32)
    ang = np.concatenate([r[:, None] * freqs, col[:, None] * freqs], axis=-1).astype(np.float32)
    return np.cos(ang).astype(np.float32), np.sin(ang).astype(np.float32)


def _core_layout(cid):
    if cid < 4:
        return cid, None, 2 * cid, 2 * cid + 1
    base = 8 + 6 * (cid - 4)
    return None, [base, base + 1, base + 2, base + 3], base + 4, base + 5


_NC_CACHE = {}
_DBG = {}


def kernel(x_prompt, x_sample, state_ret_fwd, state_ret_bwd, c, c_ctx, w_ada, b_ada, g_norm,
           w_ffn1_in, w_ffn1_out, w_in, ret_decay_logit, g_ret, w_ret_br, g_sg, b_sg, w_sp, b_sp,
           w_sg_br, w_out, w_ffn2_in, w_ffn2_out):
    f = lambda a: np.ascontiguousarray(np.asarray(a, dtype=np.float32))
    x_prompt, x_sample = f(x_prompt), f(x_sample)
    state_ret_fwd, state_ret_bwd = f(state_ret_fwd), f(state_ret_bwd)
    c, c_ctx = f(c), f(c_ctx)
    if "nc" not in _NC_CACHE:
        _NC_CACHE["nc"] = build_program()
    nc = _NC_CACHE["nc"]

    p = np.arange(128, dtype=np.float32)
    base_small = np.zeros((128, NS), np.float32)
    base_small[:, O_ID:O_ID + 128] = np.eye(128, dtype=np.float32)
    ii = np.arange(128, dtype=np.float32)
    base_small[:, O_A:O_A + 128] = np.maximum(ii[None, :] - p[:, None], 0.0)
    base_small[:, O_B:O_B + 128] = np.maximum(p[:, None] - ii[None, :], 0.0)
    base_small[:, O_R127:O_R127 + 128] = (127.0 - p)[:, None]
    base_small[:, O_RP:O_RP + 128] = p[:, None]
    base_small[:, O_RP1] = p + 1.0
    base_small[:, O_R128] = 128.0 - p
    base_small[:, O_EPS] = EPS
    base_small[:, O_ONE] = 1.0
    base_small[:, O_GN:O_GN + 48] = f(g_norm)[0].reshape(6, 8, 128).transpose(2, 0, 1).reshape(128, 48)
    base_small[:, O_BADA:O_BADA + 72] = f(b_ada)[0].reshape(72, 128).T
    base_small[:, O_GRET:O_GRET + 8] = f(g_ret)[0].reshape(8, 128).T
    base_small[:, O_GSG:O_GSG + 8] = f(g_sg)[0].reshape(8, 128).T
    base_small[:, O_BSG:O_BSG + 8] = f(b_sg)[0].reshape(8, 128).T

    rows = np.zeros((1, NR), np.float32)
    rows[0, R_LG:R_LG + 8] = f(ret_decay_logit)[0].reshape(8)
    rows[0, R_GSG:R_GSG + 1024] = f(g_sg)[0]
    rows[0, R_BSG:R_BSG + 1024] = f(b_sg)[0]
    rows[0, R_BSP:R_BSP + 512] = f(b_sp)[0].reshape(512)

    cos_s, sin_s = _rope_tables(1024)
    shared = {
        "rows": rows, "w_ada": f(w_ada)[0], "w_ffn1_in": f(w_ffn1_in)[0], "w_ffn2_in": f(w_ffn2_in)[0],
        "w_ffn1_out": f(w_ffn1_out)[0], "w_ffn2_out": f(w_ffn2_out)[0], "w_in": f(w_in)[0],
        "w_ret_br": f(w_ret_br)[0], "w_sg_br": f(w_sg_br)[0], "w_out": f(w_out)[0], "w_sp": f(w_sp)[0],
    }
    in_maps = []
    for cid in range(8):
        smp, aps, pb_, pc_ = _core_layout(cid)
        if smp is not None:
            xa = x_sample[smp]
            cA = c[smp]
        else:
            xa = np.concatenate([x_prompt[i] for i in aps], axis=0)
            cA = c_ctx
        x_in = np.ascontiguousarray(np.concatenate([xa, x_prompt[pb_], x_prompt[pc_]], axis=0))
        sm = base_small.copy()
        sm[:, O_FLAG] = 1.0 if smp is not None else 0.0
        c2 = np.stack([cA, c_ctx], axis=0)
        sm[:, O_C2:O_C2 + 16] = c2.reshape(2, 8, 128).transpose(2, 1, 0).reshape(128, 16)
        rope = np.zeros((3, 2, 128, 4, 64), np.float32)
        rope[:, 0] = 1.0
        s0 = np.zeros((2, 128, 1024), np.float32)
        if smp is not None:
            cs = cos_s.reshape(2, 4, 128, 64).transpose(0, 2, 1, 3)
            sn = sin_s.reshape(2, 4, 128, 64).transpose(0, 2, 1, 3)
            rope[0:2, 0] = cs
            rope[0:2, 1] = sn
            s0[0] = state_ret_fwd[smp, 0].transpose(1, 0, 2).reshape(128, 1024)
            s0[1] = state_ret_bwd[smp, 0].transpose(1, 0, 2).reshape(128, 1024)
        m = dict(shared)
        m.update({"x_in": x_in, "small": sm, "rope": rope, "s0": s0})
        in_maps.append(m)

    res = run_bass_kernel_spmd(nc, in_maps, core_ids=list(range(8)))
    y_prompt = np.zeros((32, 256, 1024), np.float32)
    y_sample = np.zeros((4, 1024, 1024), np.float32)
    nsf = np.zeros((32, 1, 4, 128, 256), np.float32)
    nsb = np.zeros((32, 1, 4, 128, 256), np.float32)

    def put_state(pidx, st, tile, pair):
        nsf[pidx, 0] = st[tile, pair, 0].reshape(128, 4, 256).transpose(1, 0, 2)
        nsb[pidx, 0] = st[tile, pair, 1].reshape(128, 4, 256).transpose(1, 0, 2)

    for cid in range(8):
        y = res.results[cid]["y_out"]
        st = res.results[cid]["st_out"]
        smp, aps, pb_, pc_ = _core_layout(cid)
        if smp is not None:
            y_sample[smp] = y[0:1024]
        else:
            for i, pi in enumerate(aps):
                y_prompt[pi] = y[i * 256:(i + 1) * 256]
                put_state(pi, st, i // 2, i % 2)
        y_prompt[pb_] = y[1024:1280]
        y_prompt[pc_] = y[1280:1536]
        put_state(pb_, st, 2, 0)
        put_state(pc_, st, 2, 1)
    return (y_prompt, y_sample, nsf, nsb)
```

```python
import numpy as np
from contextlib import ExitStack
import concourse.bass as bass
import concourse.mybir as mybir
from concourse.bass_utils import run_bass_kernel_spmd

F32 = mybir.dt.float32
BF16 = mybir.dt.bfloat16
AF = mybir.ActivationFunctionType
ALU = mybir.AluOpType

D = 1024
DFF = 2816
NTOK = 1536
EPS = 1e-6
import os
NSLOT = int(os.environ.get('K_NSLOT', '6'))
K_WIN = [int(v) for v in os.environ.get('K_WIN', '48,16,20').split(',')]
K_SIMONLY = os.environ.get('K_SIMONLY') == '1'
K_SAFT1 = os.environ.get('K_SAFT1', '1') == '1'
K_HT1 = os.environ.get('K_HT1', '1') == '1'
K_SCR = os.environ.get('K_SCR', '0') == '1'
K_PREN = os.environ.get('K_PREN', '1') == '1'
K_TBLPF = os.environ.get('K_TBLPF', '1') == '1'
K_DB = os.environ.get('K_DB', '1') == '1'
K_MODSPREAD = os.environ.get('K_MODSPREAD', '1') == '1'
MODPTS = (1, 3, 5) if K_MODSPREAD else ()

O_ID, O_A, O_B, O_R127, O_RP = 0, 128, 256, 384, 512
O_RP1, O_R128, O_EPS, O_ONE, O_FLAG, O_ZERO = 640, 641, 642, 643, 644, 645
O_GN, O_BADA, O_GRET, O_C2 = 648, 696, 768, 776
O_GSG, O_BSG = 792, 800
NS = 808
R_LG, R_GSG, R_BSG, R_BSP = 0, 8, 1032, 2056
NR = 2568


class Op:
    __slots__ = ("eng", "fn", "deps", "inc", "val", "sem", "waits", "is_dma", "tag", "idx", "cost", "fin", "bw")


class Sched:
    ENGS = ("pe", "act", "dve", "pool", "sp")

    def __init__(self):
        self.ops = {e: [] for e in self.ENGS}
        self.lw = {}
        self.rd = {}
        self.gdeps = []
        self.dma_cnt = {}
        self.out_dmas = []
        self.tag = "setup"
        self.alias = {}

    def add(self, eng, fn, rd=(), wr=(), dma_key=None):
        op = Op()
        op.eng, op.fn, op.inc, op.val, op.sem, op.waits = eng, fn, False, 0, None, []
        op.is_dma = dma_key is not None
        op.tag = self.tag
        op.idx = len(self.ops[eng])
        op.cost = getattr(fn, "cost", 500.0)
        op.bw = getattr(fn, "bw", 0.0)
        rd = tuple(k2 for k in rd for k2 in self.alias.get(k, (k,)))
        wr = tuple(k2 for k in wr for k2 in self.alias.get(k, (k,)))
        deps = set(self.gdeps)
        for b in rd:
            w = self.lw.get(b)
            if w is not None:
                deps.add(w)
        for b in wr:
            w = self.lw.get(b)
            if w is not None:
                deps.add(w)
            deps.update(self.rd.get(b, ()))
        op.deps = deps
        for b in rd:
            self.rd.setdefault(b, []).append(op)
        for b in wr:
            self.lw[b] = op
            self.rd[b] = []
        if op.is_dma:
            self.dma_cnt[dma_key] = self.dma_cnt.get(dma_key, 0) + 1
            op.sem = dma_key
            op.val = 16 * self.dma_cnt[dma_key]
        self.ops[eng].append(op)
        return op

    def barrier(self):
        g = []
        for e in ("pe", "act", "dve"):
            if self.ops[e]:
                g.append(self.ops[e][-1])
        self.gdeps = g

    def reorder(self, window):
        rem = {e: list(self.ops[e]) for e in self.ENGS}
        free = {e: 0.0 for e in self.ENGS}
        new = {e: [] for e in self.ENGS}
        for e in self.ENGS:
            for op in self.ops[e]:
                op.fin = None
        left = sum(len(v) for v in rem.values())
        bw_free = 0.0
        while left:
            best = None
            for e in self.ENGS:
                r = rem[e]
                if not r:
                    continue
                w = window.get(e, 1)
                for i in range(min(w, len(r))):
                    op = r[i]
                    rt = free[e]
                    ok = True
                    for d in op.deps:
                        if d.fin is None:
                            ok = False
                            break
                        lat = 60.0 if d.eng == e else 180.0
                        if d.fin + lat > rt:
                            rt = d.fin + lat
                    if not ok:
                        continue
                    cand = (rt, i, e)
                    if best is None or (rt, i) < (best[0], best[1]):
                        best = (rt, i, e)
                    if rt <= free[e]:
                        break
            assert best is not None, "scheduler deadlock"
            rt, i, e = best
            op = rem[e].pop(i)
            if op.bw:
                rt = max(rt, bw_free)
                bw_free = rt + op.bw
            op.fin = rt + op.cost
            free[e] = op.fin
            new[e].append(op)
            left -= 1
        for e in self.ENGS:
            self.ops[e] = new[e]
            for i, op in enumerate(new[e]):
                op.idx = i
        self.est_ns = max(free.values())

    def resolve(self):
        for e in self.ENGS:
            for op in self.ops[e]:
                last = {}
                for d in op.deps:
                    if d.is_dma:
                        continue
                    if d.eng == "pe" and op.eng == "pe":
                        continue
                    if d.eng not in last or d.idx > last[d.eng].idx:
                        last[d.eng] = d
                for d in last.values():
                    d.inc = True
                op.deps = set(x for x in op.deps if x.is_dma or last.get(x.eng) is x)
        for e in ("pe", "act", "dve"):
            cnt = 0
            for op in self.ops[e]:
                if op.inc:
                    cnt += 1
                    op.val = cnt
        for e in self.ENGS:
            waited = {}
            for op in self.ops[e]:
                need = {}
                for d in op.deps:
                    if d.is_dma:
                        key = ("dma", d.sem)
                    else:
                        if d.eng == "pe" and e == "pe":
                            continue
                        key = ("eng", d.eng)
                    if d.val > need.get(key, 0):
                        need[key] = d.val
                for key, v in need.items():
                    if v > waited.get(key, 0):
                        waited[key] = v
                        op.waits.append((key, v))


def build_program():
    nc = bass.Bass("TRN2", target_bir_lowering=False)

    def dram(name, shape, kind="ExternalInput"):
        return nc.dram_tensor(name, shape, F32, kind=kind).ap()

    x_in = dram("x_in", [NTOK, D])
    small_d = dram("small", [128, NS])
    rows_d = dram("rows", [1, NR])
    rope_d = dram("rope", [3, 2, 128, 4, 64])
    s0_d = dram("s0", [2, 128, 1024])
    w_ada = dram("w_ada", [D, 9 * D])
    w1 = [dram("w_ffn1_in", [D, 2 * DFF]), dram("w_ffn2_in", [D, 2 * DFF])]
    w2 = [dram("w_ffn1_out", [DFF, D]), dram("w_ffn2_out", [DFF, D])]
    w_in = dram("w_in", [D, 7168])
    w_ret = dram("w_ret_br", [D, D])
    w_sg = dram("w_sg_br", [D, D])
    w_o = dram("w_out", [D, D])
    w_sp = dram("w_sp", [4, 128, 128])
    y_out = dram("y_out", [NTOK, D], kind="ExternalOutput")
    st_out = dram("st_out", [3, 2, 2, 128, 1024], kind="ExternalOutput")

    S = Sched()
    es = ExitStack()
    with es:
        def sb(name, shape, dt=F32):
            return es.enter_context(nc.sbuf_tensor("sb_" + name, shape, dt))

        x_all = sb("x_all", [128, 8, NTOK])
        Wt = [sb(f"W{i}", [128, 8, 512], BF16) for i in range(min(NSLOT, 4) if K_SIMONLY else NSLOT)]
        while len(Wt) < NSLOT:
            Wt.append(Wt[len(Wt) % 4])
        small = sb("small", [128, NS])
        lgt = sb("lgt", [128, 8])
        lg = sb("lg", [128, 8])
        Bt = sb("Bt", [128, 8, 128])
        MT = sb("MT", [128, 512])
        kdf = sb("kdf", [128, 512])
        kdb = sb("kdb", [128, 512])
        Dq = sb("Dq", [128, 8, 128], BF16)
        identB = sb("identB", [128, 128], BF16)
        onesB = sb("onesB", [128, 128], BF16)
        wsp_n = sb("wsp_n", [128, 4, 128], BF16)
        wspT = sb("wspT", [128, 4, 128], BF16)
        cc = sb("cc", [128, 32])
        dcol = sb("dcol", [128, 8])
        cT = sb("cT", [128, 8, 2], BF16)
        mod = sb("mod", [128, 72, 2])
        Apre = sb("Apre", [128, 3, 8, 2])
        Gpost = sb("Gpost", [128, 3, 8, 2])
        SaftF = [sb(f"SaftF{i}", [128, 1024]) for i in range(1 if K_SAFT1 else 2)]
        SaftB = [sb(f"SaftB{i}", [128, 1024]) for i in range(1 if K_SAFT1 else 2)]
        if K_SAFT1:
            SaftF, SaftB = SaftF * 2, SaftB * 2
        def skey(dirn, i):
            return f"Saft{dirn}" + ("" if K_SAFT1 else str(i))
        tmpf = [sb(f"tmpf{i}", [128, 512]) for i in range(3)]
        rstd_t = sb("rstd_t", [128, 512])
        cs_t = sb("cs_t", [128, 4, 64])
        sn_t = sb("sn_t", [128, 4, 64])
        stat = sb("stat", [128, 48])
        sqb = [sb(f"sqb{i}", [128, 512], BF16) for i in range(2)]
        hTs = [sb("hTa", [128, 8, 512], BF16)]
        hTs.append(hTs[0] if K_HT1 else sb("hTb", [128, 8, 512], BF16))
        cur = {"i": 0}
        rgT = sb("rgT", [128, 8, 512], BF16)
        Sbbf = sb("Sbbf", [128, 4, 1024], BF16)
        Sfbf = sb("Sfbf", [128, 1024], BF16)
        rg = sb("rg", [128, 1024], BF16)
        UB = sb("UB", [128, 11264], BF16)
        UF = sb("UF", [128, 4096])
        VS2 = sb("VS2", [128, 1024])[:, :]
        vsn2 = sb("vsn2", [128, 1024], BF16)[:, :]
        hidden = UB[:, :].rearrange("p (j t) -> p j t", t=512)
        v_tm = UB[:, 0:4096].rearrange("p (c e) -> p c e", e=1024)
        kT = UB[:, 4096:6144].rearrange("p (c e) -> p c e", e=512)
        kfw = UB[:, 6144:8192].rearrange("p (c e) -> p c e", e=512)
        qrot = UB[:, 8192:8704]
        krot = UB[:, 8704:9216]
        qT3 = UB[:, 9216:10752].rearrange("p (a e) -> p a e", e=512)
        Pm = UB[:, 10752:11264]
        su = UB[:, 0:4096].rearrange("p (j t) -> p j t", t=512)
        ypre = UB[:, 4096:8192].rearrange("p (j t) -> p j t", t=512)
        vsn = UB[:, 8192:9216]
        kbw = UB[:, 10240:10752]
        VS = UB[:, 9216:11264].bitcast(F32)
        y_sb = UF[:, :].rearrange("p (j t) -> p j t", t=512)
        u_act = y_sb
        qf32 = UF[:, 0:512]
        kf32 = UF[:, 512:1024]
        rtmp = UF[:, 1024:2048].rearrange("p (a h e) -> p a h e", a=4, h=4)
        r_t = UF[:, 2048:3072]
        sgl = UF[:, 3072:4096]
        xs = [UF[:, 0:1024], UF[:, 1024:2048], UF[:, 2048:3072], UF[:, 3072:4096]]

        AL = S.alias
        def UFk(e):
            return (f"UF{e}a", f"UF{e}b")
        for j in range(22):
            AL[f"hid{j}"] = (f"UB{j}",)
        for c_ in range(4):
            AL[f"v{c_}"] = (f"UB{2 * c_}", f"UB{2 * c_ + 1}")
            AL[f"kT{c_}"] = (f"UB{8 + c_}",)
            AL[f"kfw{c_}"] = (f"UB{12 + c_}",)
        AL["qrot"], AL["krot"], AL["qT0"], AL["qT1"], AL["qT2"], AL["Pm"] = ("UB16",), ("UB17",), ("UB18",), ("UB19",), ("UB20",), ("UB21",)
        AL["vsn"] = ("UB16", "UB17")
        AL["VS"] = ("UB18", "UB19", "UB20", "UB21")
        AL["VS2"] = ("VS2_0", "VS2_1", "VS2_2", "VS2_3")
        for e_ in range(8):
            AL[f"sub{e_}"] = (f"UB{e_}",)
            AL[f"ypre{e_}"] = (f"UB{8 + e_}",)
            AL[f"ysb{e_}"] = UFk(e_)
        AL["qf32"], AL["kf32"] = UFk(0), UFk(1)
        for i_ in range(4):
            AL[f"rtmp{i_}"] = (f"UF{2 + i_ // 2}{'ab'[i_ % 2]}",)
            AL[f"r_t{i_}"] = (f"UF{4 + i_ // 2}{'ab'[i_ % 2]}",)
            AL[f"xs{i_}"] = UFk(2 * i_) + UFk(2 * i_ + 1)
        AL["r_t"] = UFk(4) + UFk(5)
        AL["sgl"] = UFk(6) + UFk(7)

        ps = es.enter_context(nc.psum_tensor("ps", [128, 8, 512], F32))
        rr = [0]

        NRR = 7

        def pb():
            b = rr[0] % NRR
            rr[0] += 1
            return b

        def pb2():
            while (rr[0] % NRR) % 2 or (rr[0] % NRR) + 1 >= NRR:
                rr[0] += 1
            b = rr[0] % NRR
            rr[0] += 2
            return b

        PSN = 7

        def c(o, n=1):
            return small[:, o:o + n]

        def _free(ap):
            n = 1
            for d in ap.shape[1:]:
                n *= d
            return n

        def R(name, *a, **kw):
            fn = lambda e: getattr(e, name)(*a, **kw)
            if name == "matmul":
                fn.cost = max(64, _free(a[0])) / 2.4 + 8.0
            elif name == "transpose":
                fn.cost = 120.0
            elif name == "dma_start":
                fn.bw = _free(kw["out"]) * 128 * (4 if kw["in_"].dtype == F32 else 2) / 330.0
                fn.cost = 2000.0 + fn.bw
            else:
                o = kw.get("out", a[0] if a else None)
                n = _free(o) if o is not None else 512
                fn.cost = (250.0 + n / 1.2) if name == "activation" else (150.0 + n / 0.96)
            return fn

        def A(fn, rd, wr):
            return S.add("act", fn, rd, wr)

        def V(fn, rd, wr):
            return S.add("dve", fn, rd, wr)

        def T(fn, rd, wr):
            return S.add("pe", fn, rd, wr)

        def LD(fn, key, wr, rd=()):
            return S.add("sp", fn, rd, wr, dma_key=key)

        slot_rr = [0]

        scratch = {}

        def wload(w, r0, nk, c0, ncol, reuse=K_SCR):
            s = slot_rr[0] % NSLOT
            slot_rr[0] += 1
            key = (w.name, r0, nk, c0, ncol)
            dst = Wt[s][:, 0:nk, 0:ncol]
            if reuse and key in scratch:
                scr, skey_ = scratch[key]
                S.add("sp", R("dma_start", out=dst, in_=scr.rearrange("p (k c) -> p k c", c=ncol)), (skey_,), (f"W{s}",), dma_key=f"wslot{s}")
                return s
            src = w[r0:r0 + nk * 128, c0:c0 + ncol].rearrange("(k p) c -> p k c", p=128)
            S.add("pool", R("dma_start", out=dst, in_=src), (), (f"W{s}",), dma_key=f"wslot{s}")
            if reuse:
                nm = "scr%d" % len(scratch)
                scr = nc.dram_tensor(nm, [128, nk * ncol], BF16, kind="Internal").ap()
                scratch[key] = (scr, nm)
                S.add("sp", R("dma_start", out=scr.rearrange("p (k c) -> p k c", c=ncol), in_=dst), (f"W{s}",), (nm,), dma_key=f"scrst{s}")
            return s

        LD(R("dma_start", out=small[:], in_=small_d), "ld_small", ("small", "smallAB"))
        LD(R("dma_start", out=lgt[:], in_=rows_d[:, R_LG:R_LG + 8].partition_broadcast(128)), "ld_rows", ("lgt",))
        LD(R("dma_start", out=tmpf[1][:], in_=rows_d[:, R_BSP:R_BSP + 512].partition_broadcast(128)), "ld_rows4", ("tmpf1",))
        S.add("pool", R("dma_start", out=wsp_n[:], in_=w_sp.rearrange("g i j -> i g j")), (), ("wsp_n",), dma_key="ld_wsp")

        V(R("tensor_copy", out=identB[:], in_=c(O_ID, 128)), ("small",), ("identB",))
        V(R("memset", onesB[:], 1.0), (), ("onesB",))
        V(R("memset", cc[:], 0.0), (), ("cc",))
        A(R("activation", out=lg[:], in_=lgt[:], func=AF.Exp, scale=-1.0), ("lgt",), ("lg",))
        A(R("activation", out=lg[:], in_=lg[:], func=AF.Ln, bias=c(O_ONE), scale=1.0), ("lg", "small"), ("lg",))
        V(R("tensor_scalar", out=lg[:], in0=lg[:], scalar1=-1.0, scalar2=0.0, op0=ALU.mult, op1=ALU.add), ("lg",), ("lg",))
        V(R("tensor_copy", out=cc[:, 0:1], in_=c(O_ONE)), ("small", "cc"), ("cc",))
        V(R("tensor_copy", out=cc[:, 1:2], in_=c(O_FLAG)), ("small", "cc"), ("cc",))
        A(R("activation", out=cc[:, 8:16], in_=lg[:], func=AF.Exp, scale=128.0), ("lg", "cc"), ("cc",))
        V(R("tensor_scalar", out=cc[:, 16:24], in0=cc[:, 8:16], scalar1=c(O_FLAG), scalar2=0.0, op0=ALU.mult, op1=ALU.add), ("cc", "small"), ("cc",))
        for h in range(4):
            hs = slice(h * 128, (h + 1) * 128)
            V(R("tensor_scalar", out=tmpf[0][:, 0:128], in0=c(O_A, 128), scalar1=lg[:, h:h + 1], scalar2=0.0, op0=ALU.mult, op1=ALU.add), ("lg", "small", "smallAB"), ("tmpf0",))
            V(R("scalar_tensor_tensor", out=tmpf[0][:, 128:256], in0=c(O_B, 128), scalar=lg[:, 4 + h:5 + h], in1=tmpf[0][:, 0:128], op0=ALU.mult, op1=ALU.add), ("lg", "small", "smallAB", "tmpf0"), ("tmpf0",))
            A(R("activation", out=tmpf[0][:, 256:384], in_=tmpf[0][:, 128:256], func=AF.Exp), ("tmpf0",), ("tmpf0",))
            V(R("tensor_tensor", out=MT[:, hs], in0=tmpf[0][:, 256:384], in1=c(O_ID, 128), op=ALU.add), ("tmpf0", "small"), ("MT",))
            A(R("activation", out=kdf[:, hs], in_=c(O_R127, 128), func=AF.Exp, scale=lg[:, h:h + 1]), ("lg", "small", "smallAB"), ("kdf",))
            A(R("activation", out=kdb[:, hs], in_=c(O_RP, 128), func=AF.Exp, scale=lg[:, 4 + h:5 + h]), ("lg", "small", "smallAB"), ("kdb",))
            A(R("activation", out=dcol[:, h:h + 1], in_=c(O_RP1), func=AF.Exp, scale=lg[:, h:h + 1]), ("lg", "small"), ("dcol",))
            A(R("activation", out=dcol[:, 4 + h:5 + h], in_=c(O_R128), func=AF.Exp, scale=lg[:, 4 + h:5 + h]), ("lg", "small"), ("dcol",))
            V(R("tensor_scalar", out=Dq[:, h, :], in0=c(O_ID, 128), scalar1=dcol[:, h:h + 1], scalar2=0.0, op0=ALU.mult, op1=ALU.add), ("dcol", "small"), ("Dq",))
            V(R("tensor_scalar", out=Dq[:, 4 + h, :], in0=c(O_ID, 128), scalar1=dcol[:, 4 + h:5 + h], scalar2=0.0, op0=ALU.mult, op1=ALU.add), ("dcol", "small"), ("Dq",))
        PSM = pb()
        for g in range(4):
            T(R("matmul", ps[:, PSM, g * 128:(g + 1) * 128], lhsT=wsp_n[:, g, :], rhs=identB[:], start=True, stop=True), ("wsp_n", "identB"), (f"ps{PSM}",))
        A(R("activation", out=wspT[:].rearrange("p g i -> p (g i)"), in_=ps[:, PSM, :], func=AF.Copy), (f"ps{PSM}",), ("wspT",))
        PSR = pb()
        for g in range(4):
            T(R("matmul", ps[:, PSR, g * 128:(g + 1) * 128], lhsT=onesB[:], rhs=wspT[:, g, :], start=True, stop=True), ("onesB", "wspT"), (f"ps{PSR}",))
        for e8 in range(8):
            g = e8 // 2
            V(R("scalar_tensor_tensor", out=Bt[:, e8, :], in0=ps[:, PSR, g * 128:(g + 1) * 128], scalar=c(O_BSG + e8), in1=tmpf[1][:, g * 128:(g + 1) * 128], op0=ALU.mult, op1=ALU.add),
              (f"ps{PSR}", "small", "tmpf1"), ("Bt",))

        def load_x(c12, extra_rd=()):
            st = xs[c12 % 4]
            key = f"xs{c12 % 4}"
            LD(R("dma_start", out=st, in_=x_in[c12 * 128:(c12 + 1) * 128, :]), f"ld_x{c12 % 4}", (key,), rd=extra_rd)
            t = c12 // 4
            tok = slice(c12 * 128, (c12 + 1) * 128)
            for half in range(2):
                b = pb()
                for j in range(4):
                    k = half * 4 + j
                    T(R("transpose", out=ps[:, b, j * 128:(j + 1) * 128], in_=st[:, k * 128:(k + 1) * 128], identity=c(O_ID, 128)),
                      (key, "small"), (f"ps{b}",))
                wr = tuple(f"x{half * 4 + j}_{t}" for j in range(4))
                eng = A if half == 0 else V
                if half == 0:
                    A(R("activation", out=x_all[:, half * 4:half * 4 + 4, tok], in_=ps[:, b, :].rearrange("p (a q) -> p a q", q=128), func=AF.Copy), (f"ps{b}",), wr)
                else:
                    V(R("tensor_copy", out=x_all[:, half * 4:half * 4 + 4, tok], in_=ps[:, b, :].rearrange("p (a q) -> p a q", q=128)), (f"ps{b}",), wr)

        A(R("activation", out=cT[:].rearrange("p k s -> p (k s)"), in_=c(O_C2, 16), func=AF.Silu), ("small",), ("cT",))
        modv = mod[:].rearrange("p (i k) s -> p i k s", k=8)
        gnv = c(O_GN, 48).rearrange("p (n k) -> p n k", k=8)

        modq = list(range(18))

        def mod_block():
            if not modq:
                return
            blk = modq.pop(0)
            L, i_ = blk // 6, (blk % 6) // 2
            tag = S.tag
            S.tag = f"mod{L}"
            s_ = wload(w_ada, 0, 8, blk * 512, 512, reuse=False)
            PSM = pb()
            for m in range(4):
                for k in range(8):
                    T(R("matmul", ps[:, PSM, m * 2:m * 2 + 2], lhsT=Wt[s_][:, k, m * 128:(m + 1) * 128], rhs=cT[:, k, :], start=(k == 0), stop=(k == 7)),
                      (f"W{s_}", "cT"), (f"ps{PSM}",))
            j0 = blk * 4
            V(R("tensor_tensor", out=mod[:, j0:j0 + 4, :], in0=ps[:, PSM, 0:8].rearrange("p (j s) -> p j s", s=2),
                in1=c(O_BADA + j0, 4).unsqueeze(2).to_broadcast([128, 4, 2]), op=ALU.add), (f"ps{PSM}", "small"), (f"mod{L}_{i_}",))
            if blk % 2 == 1:
                coef = 1.0 if L == 1 else 0.5
                if i_ == 1:
                    V(R("scalar_tensor_tensor", out=Apre[:, L], in0=modv[:, 3 * L + 1], scalar=1.0, in1=gnv[:, 2 * L].unsqueeze(2).to_broadcast([128, 8, 2]), op0=ALU.add, op1=ALU.mult),
                      (f"mod{L}_1", "small"), (f"Apre{L}",))
                if i_ == 2:
                    V(R("scalar_tensor_tensor", out=Gpost[:, L], in0=modv[:, 3 * L + 2], scalar=coef, in1=gnv[:, 2 * L + 1].unsqueeze(2).to_broadcast([128, 8, 2]), op0=ALU.mult, op1=ALU.mult),
                      (f"mod{L}_2", "small"), (f"Gpost{L}",))
            S.tag = tag

        for c12 in range(4):
            load_x(c12)
        for i_ in range(6):
            mod_block()
            if i_ == 3:
                for c12 in range(4, 12):
                    load_x(c12, extra_rd=(f"W{(slot_rr[0] - 1) % NSLOT}",))

        def xk(k, t):
            return f"x{k}_{t}"

        def rstd_from_psn(n_inv):
            A(R("activation", out=rstd_t[:], in_=ps[:, PSN, :], func=AF.Sqrt, bias=c(O_EPS), scale=n_inv), ("ps7", "small"), ("rstd_t",))
            V(R("reciprocal", out=rstd_t[:], in_=rstd_t[:]), ("rstd_t",), ("rstd_t",))

        rstd_p = small[:, O_A:O_A + 512]
        NSEQ = [(0, 0), (0, 1), (0, 2), (1, 1), (1, 0), (1, 1), (1, 2), (2, 0), (2, 1), (2, 2)]
        nptr = {"made": 0, "used": 0, "res": {}}

        def _norm_pre_emit(L, t):
            cur["i"] ^= 1
            hT, hp = hTs[cur["i"]], "hT%d_" % (0 if K_HT1 else cur["i"])
            ts = slice(t * 512, (t + 1) * 512)
            sl = 0 if t < 2 else 1
            bn = pb()
            for k in range(8):
                q = sqb[k % 2]
                A(R("activation", out=q[:], in_=x_all[:, k, ts], func=AF.Square), (xk(k, t),), (f"sqb{k % 2}",))
                T(R("matmul", ps[:, bn, :], lhsT=onesB[:], rhs=q[:], start=(k == 0), stop=(k == 7)), (f"sqb{k % 2}", "onesB"), (f"ps{bn}",))
            A(R("activation", out=rstd_p, in_=ps[:, bn, :], func=AF.Sqrt, bias=c(O_EPS), scale=1.0 / D), (f"ps{bn}", "small"), ("smallAB",))
            V(R("reciprocal", out=rstd_p, in_=rstd_p), ("smallAB",), ("smallAB",))
            for k in range(8):
                tf = tmpf[k % 3]
                V(R("scalar_tensor_tensor", out=tf[:], in0=x_all[:, k, ts], scalar=Apre[:, L, k, sl:sl + 1], in1=rstd_p, op0=ALU.mult, op1=ALU.mult),
                  (xk(k, t), f"Apre{L}", "smallAB"), (f"tmpf{k % 3}",))
                A(R("activation", out=hT[:, k, :], in_=tf[:], func=AF.Identity, bias=modv[:, 3 * L, k, sl:sl + 1], scale=1.0),
                  (f"tmpf{k % 3}", f"mod{L}_0"), (hp + str(k),))
            return hT, hp

        def prefetch_norm():
            i = nptr["made"]
            if i < len(NSEQ) and i == nptr["used"] and K_PREN:
                tag = S.tag
                nptr["res"][i] = _norm_pre_emit(*NSEQ[i])
                nptr["made"] = i + 1
                S.tag = tag

        def norm_pre(L, t):
            i = nptr["used"]
            assert NSEQ[i] == (L, t), (NSEQ[i], L, t)
            if nptr["made"] == i:
                nptr["res"][i] = _norm_pre_emit(L, t)
                nptr["made"] = i + 1
            nptr["used"] = i + 1
            return nptr["res"].pop(i)


        def post_norm(L, t):
            ts = slice(t * 512, (t + 1) * 512)
            sl = 0 if t < 2 else 1
            rstd_from_psn(1.0 / D)
            for k in range(8):
                tf = tmpf[k % 3]
                V(R("scalar_tensor_tensor", out=tf[:], in0=y_sb[:, k, :], scalar=Gpost[:, L, k, sl:sl + 1], in1=rstd_t[:], op0=ALU.mult, op1=ALU.mult),
                  (f"ysb{k}", f"Gpost{L}", "rstd_t"), (f"tmpf{k % 3}",))
                V(R("tensor_tensor", out=x_all[:, k, ts], in0=x_all[:, k, ts], in1=tf[:], op=ALU.add),
                  (f"tmpf{k % 3}", xk(k, t)), (xk(k, t),))

        def out_proj_rows(wd, nkc, rhs_fn, rhs_keys):
            kblocks = [(i, min(8, nkc - i)) for i in range(0, nkc, 8)]
            first_sq = [True]
            for cb in range(2):
                banks = [pb() for _ in range(4)]
                for (k0, nk) in kblocks:
                    s = wload(wd, k0 * 128, nk, cb * 512, 512)
                    for m in range(4):
                        for k in range(nk):
                            kk = k0 + k
                            T(R("matmul", ps[:, banks[m], :], lhsT=Wt[s][:, k, m * 128:(m + 1) * 128], rhs=rhs_fn(kk), start=(kk == 0), stop=(kk == nkc - 1)),
                              (f"W{s}",) + rhs_keys(kk), (f"ps{banks[m]}",))
                for m in range(4):
                    e_ = cb * 4 + m
                    b = banks[m]
                    q = sqb[e_ % 2]
                    A(R("activation", out=y_sb[:, e_, :], in_=ps[:, b, :], func=AF.Copy), (f"ps{b}",), (f"ysb{e_}",))
                    A(R("activation", out=q[:], in_=ps[:, b, :], func=AF.Square), (f"ps{b}",), (f"sqb{e_ % 2}",))
                    T(R("matmul", ps[:, PSN, :], lhsT=onesB[:], rhs=q[:], start=(e_ == 0), stop=(e_ == 7)), (f"sqb{e_ % 2}", "onesB"), ("ps7",))

        def ffn(L, t):
            fi = 0 if L == 0 else 1
            S.tag = f"ffn{L}_t{t}"
            hT, hp = norm_pre(L, t)
            for cb in range(6):
                ncol = 512 if cb < 5 else 256
                sa = wload(w1[fi], 0, 8, cb * 512, ncol)
                sb_ = wload(w1[fi], 0, 8, DFF + cb * 512, ncol)
                for m in range(ncol // 128):
                    j = cb * 4 + m
                    ba, bb = pb(), pb()
                    for k in range(8):
                        T(R("matmul", ps[:, ba, :], lhsT=Wt[sa][:, k, m * 128:(m + 1) * 128], rhs=hT[:, k, :], start=(k == 0), stop=(k == 7)),
                          (f"W{sa}", hp + str(k)), (f"ps{ba}",))
                    for k in range(8):
                        T(R("matmul", ps[:, bb, :], lhsT=Wt[sb_][:, k, m * 128:(m + 1) * 128], rhs=hT[:, k, :], start=(k == 0), stop=(k == 7)),
                          (f"W{sb_}", hp + str(k)), (f"ps{bb}",))
                    tf = tmpf[j % 3]
                    A(R("activation", out=tf[:], in_=ps[:, ba, :], func=AF.Silu), (f"ps{ba}",), (f"tmpf{j % 3}",))
                    V(R("tensor_tensor", out=hidden[:, j, :], in0=tf[:], in1=ps[:, bb, :], op=ALU.mult), (f"ps{bb}", f"tmpf{j % 3}"), (f"hid{j}",))
                if L == 0 and cb in MODPTS and len(modq) > 6:
                    mod_block()
            prefetch_norm()
            out_proj_rows(w2[fi], 22, lambda kk: hidden[:, kk, :], lambda kk: (f"hid{kk}",))
            post_norm(L, t)

        def rope_ops(src, dst, cidx, rd_key, wr_key):
            sv = src.rearrange("p (h two d) -> p h two d", h=4, two=2)
            dv = dst.rearrange("p (h two d) -> p h two d", h=4, two=2)
            x1, x2 = sv[:, :, 0, :], sv[:, :, 1, :]
            cs = cs_t[:, cidx:cidx + 1, :].to_broadcast([128, 4, 64])
            sn = sn_t[:, cidx:cidx + 1, :].to_broadcast([128, 4, 64])
            V(R("tensor_tensor", out=rtmp[:, 0], in0=x1, in1=cs, op=ALU.mult), (rd_key, "rope"), ("rtmp0",))
            V(R("tensor_tensor", out=rtmp[:, 1], in0=x2, in1=sn, op=ALU.mult), (rd_key, "rope"), ("rtmp1",))
            V(R("tensor_tensor", out=dv[:, :, 0, :], in0=rtmp[:, 0], in1=rtmp[:, 1], op=ALU.subtract), ("rtmp0", "rtmp1"), (wr_key,))
            V(R("tensor_tensor", out=rtmp[:, 2], in0=x1, in1=sn, op=ALU.mult), (rd_key, "rope"), ("rtmp2",))
            V(R("tensor_tensor", out=rtmp[:, 3], in0=x2, in1=cs, op=ALU.mult), (rd_key, "rope"), ("rtmp3",))
            V(R("tensor_tensor", out=dv[:, :, 1, :], in0=rtmp[:, 2], in1=rtmp[:, 3], op=ALU.add), ("rtmp2", "rtmp3"), (wr_key,))

        def stats_small(o_sum, o_ssq, o_out, ncol, n, sqrt_rd=()):
            sm = stat[:, o_sum:o_sum + ncol]
            sq_ = stat[:, o_ssq:o_ssq + ncol]
            mean = stat[:, 32:32 + ncol]
            m2 = stat[:, 36:36 + ncol]
            var = stat[:, 40:40 + ncol]
            sd = stat[:, 44:44 + ncol]
            rstd = stat[:, o_out:o_out + ncol]
            nmr = stat[:, o_out + ncol:o_out + 2 * ncol]
            V(R("tensor_scalar", out=mean, in0=sm, scalar1=1.0 / n, scalar2=0.0, op0=ALU.mult, op1=ALU.add), ("stat",), ("stat",))
            V(R("tensor_tensor", out=m2, in0=mean, in1=mean, op=ALU.mult), ("stat",), ("stat",))
            V(R("scalar_tensor_tensor", out=var, in0=sq_, scalar=1.0 / n, in1=m2, op0=ALU.mult, op1=ALU.subtract), ("stat",), ("stat",))
            V(R("tensor_scalar", out=var, in0=var, scalar1=0.0, scalar2=0.0, op0=ALU.max, op1=ALU.add), ("stat",), ("stat",))
            A(R("activation", out=sd, in_=var, func=AF.Sqrt, bias=c(O_EPS), scale=1.0), ("stat", "small") + tuple(sqrt_rd), ("stat",))
            V(R("reciprocal", out=rstd, in_=sd), ("stat",), ("stat",))
            V(R("scalar_tensor_tensor", out=nmr, in0=mean, scalar=-1.0, in1=rstd, op0=ALU.mult, op1=ALU.mult), ("stat",), ("stat",))

        saft_i = {"f": 0, "b": 0}

        def carry_type(t, cfrom, cto):
            lo = min(cfrom, cto)
            if lo == 1:
                return 1 if t < 2 else 2
            return 0

        def state_step(dirn, t, cidx, prev_ap, prev_key, ct, psb, out_pair):
            d = 0 if dirn == "f" else 1
            bufs = SaftF if dirn == "f" else SaftB
            ni = saft_i[dirn]
            saft_i[dirn] = 1 - ni
            newb = bufs[ni]
            nkey = skey(dirn, ni)
            if dirn == "f":
                V(R("tensor_scalar", out=Sfbf[:], in0=prev_ap, scalar1=cc[:, ct:ct + 1], scalar2=0.0, op0=ALU.mult, op1=ALU.add), (prev_key, "cc"), ("Sfbf",))
            else:
                A(R("activation", out=Sbbf[:, cidx, :], in_=prev_ap, func=AF.Identity, scale=cc[:, ct:ct + 1]), (prev_key, "cc"), (f"Sbbf{cidx}",))
            for h in range(4):
                col = 8 + ct * 8 + d * 4 + h
                bank = psb + h // 2
                V(R("scalar_tensor_tensor", out=newb[:, h * 256:(h + 1) * 256], in0=prev_ap[:, h * 256:(h + 1) * 256], scalar=cc[:, col:col + 1],
                                                                          in1=ps[:, bank, (h % 2) * 256:(h % 2) * 256 + 256], op0=ALU.mult, op1=ALU.add),
                  (prev_key, "cc", f"ps{bank}"), (nkey,))
            if out_pair is not None:
                S.out_dmas.append(LD(R("dma_start", out=st_out[t, out_pair, d], in_=newb[:]), f"st_{dirn}{ni}", (), rd=(nkey,)))
            return newb[:], nkey

        state = {}

        def mixer_M12(t, full=True):
            S.tag = f"M12_t{t}" + ("" if full else "pre")
            hT, hp = norm_pre(1, t)
            state["hT"] = (hT, hp)
            LD(R("dma_start", out=cs_t[:], in_=rope_d[t, 0]), "ld_rope", ("rope",))
            LD(R("dma_start", out=sn_t[:], in_=rope_d[t, 1]), "ld_rope", ("rope",))
            sk = wload(w_in, 0, 8, 512, 512)
            sv0 = wload(w_in, 0, 8, 1024, 512)
            sv1 = wload(w_in, 0, 8, 1536, 512)
            pi_ = 1 - saft_i["b"]
            if t == 1:
                LD(R("dma_start", out=SaftB[pi_][:], in_=s0_d[1]), "ld_s0b", (skey("b", pi_),))
                prev, pkey, ct = SaftB[pi_][:], skey("b", pi_), 0
            elif t == 0:
                prev, pkey, ct = state["bX"][0], state["bX"][1], 1
            else:
                prev, pkey, ct = SaftB[pi_][:], skey("b", pi_), 2
            for cidx in (3, 2, 1, 0):
                tk = slice(cidx * 128, (cidx + 1) * 128)
                bk = pb()
                bv = pb2()
                for k in range(8):
                    T(R("matmul", ps[:, bk, :], lhsT=hT[:, k, tk], rhs=Wt[sk][:, k, :], start=(k == 0), stop=(k == 7)), (f"W{sk}", hp + str(k)), (f"ps{bk}",))
                for k in range(8):
                    T(R("matmul", ps[:, bv, :], lhsT=hT[:, k, tk], rhs=Wt[sv0][:, k, :], start=(k == 0), stop=(k == 7)), (f"W{sv0}", hp + str(k)), (f"ps{bv}",))
                for k in range(8):
                    T(R("matmul", ps[:, bv + 1, :], lhsT=hT[:, k, tk], rhs=Wt[sv1][:, k, :], start=(k == 0), stop=(k == 7)), (f"W{sv1}", hp + str(k)), (f"ps{bv + 1}",))
                A(R("activation", out=kf32, in_=ps[:, bk, :], func=AF.Copy), (f"ps{bk}",), ("kf32",))
                A(R("activation", out=v_tm[:, cidx, :].rearrange("p (a q) -> p a q", q=512), in_=ps[:, bv:bv + 2, :], func=AF.Copy), (f"ps{bv}", f"ps{bv + 1}"), (f"v{cidx}",))
                rope_ops(kf32, krot, cidx, "kf32", "krot")
                if full:
                    V(R("tensor_tensor", out=kfw[:, cidx, :], in0=krot, in1=kdf[:], op=ALU.mult), ("krot", "kdf"), (f"kfw{cidx}",))
                    bt = pb()
                    for h in range(4):
                        T(R("matmul", ps[:, bt, h * 128:(h + 1) * 128], lhsT=krot[:, h * 128:(h + 1) * 128], rhs=identB[:], start=True, stop=True), ("krot", "identB"), (f"ps{bt}",))
                    A(R("activation", out=kT[:, cidx, :], in_=ps[:, bt, :], func=AF.Copy), (f"ps{bt}",), (f"kT{cidx}",))
                V(R("tensor_tensor", out=kbw, in0=krot, in1=kdb[:], op=ALU.mult), ("krot", "kdb"), ("qT2",))
                bs_ = pb2()
                for h in range(4):
                    T(R("matmul", ps[:, bs_ + h // 2, (h % 2) * 256:(h % 2) * 256 + 256], lhsT=kbw[:, h * 128:(h + 1) * 128], rhs=v_tm[:, cidx, h * 256:(h + 1) * 256], start=True, stop=True),
                      ("qT2", f"v{cidx}"), (f"ps{bs_ + h // 2}",))
                prev, pkey = state_step("b", t, cidx, prev, pkey, ct, bs_, (cidx // 2) if (full and cidx in (0, 2)) else None)
                if cidx > 0:
                    ct = carry_type(t, cidx, cidx - 1)
            state["bX"] = (prev, pkey)
            if not full:
                prefetch_norm()

        def mixer_rest(t):
            ts = slice(t * 512, (t + 1) * 512)
            hT, hp = state["hT"]
            S.tag = f"M3_t{t}"
            sq_ = wload(w_in, 0, 8, 0, 512)
            sg0 = wload(w_in, 0, 8, 2048, 512)
            sg1 = wload(w_in, 0, 8, 2560, 512)
            pi_ = 1 - saft_i["f"]
            if t == 0:
                LD(R("dma_start", out=SaftF[pi_][:], in_=s0_d[0]), "ld_s0f", (skey("f", pi_),))
                prev, pkey, ct = SaftF[pi_][:], skey("f", pi_), 0
            elif t == 1:
                prev, pkey, ct = state["fX"][0], state["fX"][1], 1
            else:
                prev, pkey, ct = SaftF[pi_][:], skey("f", pi_), 2
            for cidx in range(4):
                tk = slice(cidx * 128, (cidx + 1) * 128)
                bq = pb()
                bg = pb2()
                for k in range(8):
                    T(R("matmul", ps[:, bq, :], lhsT=hT[:, k, tk], rhs=Wt[sq_][:, k, :], start=(k == 0), stop=(k == 7)), (f"W{sq_}", hp + str(k)), (f"ps{bq}",))
                for k in range(8):
                    T(R("matmul", ps[:, bg, :], lhsT=hT[:, k, tk], rhs=Wt[sg0][:, k, :], start=(k == 0), stop=(k == 7)), (f"W{sg0}", hp + str(k)), (f"ps{bg}",))
                for k in range(8):
                    T(R("matmul", ps[:, bg + 1, :], lhsT=hT[:, k, tk], rhs=Wt[sg1][:, k, :], start=(k == 0), stop=(k == 7)), (f"W{sg1}", hp + str(k)), (f"ps{bg + 1}",))
                A(R("activation", out=qf32, in_=ps[:, bq, :], func=AF.Identity, scale=128.0 ** -0.5), (f"ps{bq}",), ("qf32",))
                A(R("activation", out=sgl.rearrange("p (a q) -> p a q", q=512), in_=ps[:, bg:bg + 2, :], func=AF.Silu), (f"ps{bg}", f"ps{bg + 1}"), ("sgl",))
                rope_ops(qf32, qrot, cidx, "qf32", "qrot")
                b3 = [pb(), pb(), pb()]
                for h in range(4):
                    hs = slice(h * 128, (h + 1) * 128)
                    T(R("matmul", ps[:, b3[0], hs], lhsT=qrot[:, hs], rhs=identB[:], start=True, stop=True), ("qrot", "identB"), (f"ps{b3[0]}",))
                    T(R("matmul", ps[:, b3[1], hs], lhsT=qrot[:, hs], rhs=Dq[:, h, :], start=True, stop=True), ("qrot", "Dq"), (f"ps{b3[1]}",))
                    T(R("matmul", ps[:, b3[2], hs], lhsT=qrot[:, hs], rhs=Dq[:, 4 + h, :], start=True, stop=True), ("qrot", "Dq"), (f"ps{b3[2]}",))
                A(R("activation", out=qT3[:, 0, :], in_=ps[:, b3[0], :], func=AF.Copy), (f"ps{b3[0]}",), ("qT0",))
                V(R("tensor_copy", out=qT3[:, 1, :], in_=ps[:, b3[1], :]), (f"ps{b3[1]}",), ("qT1",))
                A(R("activation", out=qT3[:, 2, :], in_=ps[:, b3[2], :], func=AF.Copy), (f"ps{b3[2]}",), ("qT2",))
                if K_TBLPF:
                    A(R("activation", out=stat[:, 24:25], in_=c(O_ONE), func=AF.Sqrt), ("qT2", "sgl", "small"), ("statpf",))
                bp = pb()
                for h in range(4):
                    hs = slice(h * 128, (h + 1) * 128)
                    T(R("matmul", ps[:, bp, hs], lhsT=kT[:, cidx, hs], rhs=qT3[:, 0, hs], start=True, stop=True), (f"kT{cidx}", "qT0"), (f"ps{bp}",))
                V(R("tensor_tensor", out=Pm, in0=ps[:, bp, :], in1=MT[:], op=ALU.mult), (f"ps{bp}", "MT"), ("Pm",))
                bsf = pb2()
                for h in range(4):
                    T(R("matmul", ps[:, bsf + h // 2, (h % 2) * 256:(h % 2) * 256 + 256], lhsT=kfw[:, cidx, h * 128:(h + 1) * 128], rhs=v_tm[:, cidx, h * 256:(h + 1) * 256], start=True, stop=True),
                      (f"kfw{cidx}", f"v{cidx}"), (f"ps{bsf + h // 2}",))
                prev, pkey = state_step("f", t, cidx, prev, pkey, ct, bsf, (cidx // 2) if cidx in (1, 3) else None)
                if cidx < 3:
                    ct = carry_type(t, cidx, cidx + 1)
                bo = pb2()
                for h in range(4):
                    hs = slice(h * 128, (h + 1) * 128)
                    es_ = slice(h * 256, (h + 1) * 256)
                    oo = ps[:, bo + h // 2, (h % 2) * 256:(h % 2) * 256 + 256]
                    T(R("matmul", oo, lhsT=Pm[:, hs], rhs=v_tm[:, cidx, es_], start=True, stop=False), ("Pm", f"v{cidx}"), (f"ps{bo + h // 2}",))
                    T(R("matmul", oo, lhsT=qT3[:, 1, hs], rhs=Sfbf[:, es_], start=False, stop=False), ("qT1", "Sfbf"), (f"ps{bo + h // 2}",))
                    T(R("matmul", oo, lhsT=qT3[:, 2, hs], rhs=Sbbf[:, cidx, es_], start=False, stop=True), ("qT2", f"Sbbf{cidx}"), (f"ps{bo + h // 2}",))
                for h in range(4):
                    oo = ps[:, bo + h // 2, (h % 2) * 256:(h % 2) * 256 + 256]
                    A(R("activation", out=r_t[:, h * 256:(h + 1) * 256], in_=oo, func=AF.Copy, accum_out=stat[:, h:h + 1]), (f"ps{bo + h // 2}",), (f"r_t{h}", "stat"))
                    A(R("activation", out=rg[:, h * 256:(h + 1) * 256], in_=oo, func=AF.Square, accum_out=stat[:, 4 + h:5 + h]), (f"ps{bo + h // 2}",), ("rg", "stat"))
                stats_small(0, 4, 8, 4, 256.0)
                for h in range(4):
                    A(R("activation", out=r_t[:, h * 256:(h + 1) * 256], in_=r_t[:, h * 256:(h + 1) * 256], func=AF.Identity, scale=stat[:, 8 + h:9 + h], bias=stat[:, 12 + h:13 + h]),
                      (f"r_t{h}", "stat"), (f"r_t{h}",))
                V(R("tensor_tensor", out=rg[:], in0=r_t, in1=sgl, op=ALU.mult), ("r_t", "sgl"), ("rg",))
                bt = pb2()
                for e8 in range(8):
                    T(R("matmul", ps[:, bt + e8 // 4, (e8 % 4) * 128:(e8 % 4) * 128 + 128], lhsT=rg[:, e8 * 128:(e8 + 1) * 128], rhs=identB[:], start=True, stop=True),
                      ("rg", "identB"), (f"ps{bt + e8 // 4}",))
                for half in range(2):
                    V(R("tensor_tensor", out=rgT[:, half * 4:half * 4 + 4, tk], in0=ps[:, bt + half, :].rearrange("p (a q) -> p a q", q=128),
                                                           in1=c(O_GRET + half * 4, 4).unsqueeze(2).to_broadcast([128, 4, 128]), op=ALU.mult),
                      (f"ps{bt + half}", "small"), (f"rgT{cidx}",))
            state["fX"] = (prev, pkey)
            if len(modq) > 0 and K_MODSPREAD:
                mod_block()
                mod_block()
            S.tag = f"M4_t{t}"
            su0 = wload(w_in, 0, 8, 3072, 512)
            su1 = wload(w_in, 0, 8, 3584, 512)
            for blk, s in ((0, su0), (1, su1)):
                for m in range(4):
                    e_ = blk * 4 + m
                    b = pb()
                    for k in range(8):
                        T(R("matmul", ps[:, b, :], lhsT=Wt[s][:, k, m * 128:(m + 1) * 128], rhs=hT[:, k, :], start=(k == 0), stop=(k == 7)), (f"W{s}", hp + str(k)), (f"ps{b}",))
                    A(R("activation", out=u_act[:, e_, :], in_=ps[:, b, :], func=AF.Gelu_apprx_tanh), (f"ps{b}",), (f"ysb{e_}",))
            sv0 = wload(w_in, 0, 8, 4096, 512)
            sv1 = wload(w_in, 0, 8, 4608, 512)
            for pr in range(2):
                info = {}
                for cidx in (2 * pr, 2 * pr + 1):
                    tk = slice(cidx * 128, (cidx + 1) * 128)
                    par = cidx % 2
                    VS_, vsn_ = (VS, VS2)[par], (vsn, vsn2)[par]
                    KVS, Kvsn = ("VS", "VS2")[par], ("vsn", "vsn2")[par]
                    so = 16 + 4 * par
                    bv = pb2()
                    for k in range(8):
                        T(R("matmul", ps[:, bv, :], lhsT=hT[:, k, tk], rhs=Wt[sv0][:, k, :], start=(k == 0), stop=(k == 7)), (f"W{sv0}", hp + str(k)), (f"ps{bv}",))
                    for k in range(8):
                        T(R("matmul", ps[:, bv + 1, :], lhsT=hT[:, k, tk], rhs=Wt[sv1][:, k, :], start=(k == 0), stop=(k == 7)), (f"W{sv1}", hp + str(k)), (f"ps{bv + 1}",))
                    A(R("activation", out=VS_.rearrange("p (a q) -> p a q", q=512), in_=ps[:, bv:bv + 2, :], func=AF.Gelu_apprx_tanh, accum_out=stat[:, so:so + 1]), (f"ps{bv}", f"ps{bv + 1}"), (KVS, "stat"))
                    A(R("activation", out=vsn_, in_=VS_, func=AF.Square, accum_out=stat[:, so + 1:so + 2]), (KVS,), (Kvsn, "stat"))
                    info[cidx] = (tk, par, VS_, vsn_, KVS, Kvsn, so)
                for cidx in (2 * pr, 2 * pr + 1):
                    tk, par, VS_, vsn_, KVS, Kvsn, so = info[cidx]
                    other = info[2 * pr + 1][5] if cidx == 2 * pr else ()
                    stats_small(so, so + 1, so + 2, 1, 1024.0, sqrt_rd=((other,) if other else ()))
                for cidx in (2 * pr, 2 * pr + 1):
                    tk, par, VS_, vsn_, KVS, Kvsn, so = info[cidx]
                    A(R("activation", out=vsn_, in_=VS_, func=AF.Identity, scale=stat[:, so + 2:so + 3], bias=stat[:, so + 3:so + 4]), (KVS, "stat"), (Kvsn,))
                    bsp = pb2()
                    for e8 in range(8):
                        T(R("matmul", ps[:, bsp + e8 // 4, (e8 % 4) * 128:(e8 % 4) * 128 + 128], lhsT=vsn_[:, e8 * 128:(e8 + 1) * 128], rhs=wspT[:, e8 // 2, :], start=True, stop=True),
                          (Kvsn, "wspT"), (f"ps{bsp + e8 // 4}",))
                    for e8 in range(8):
                        V(R("scalar_tensor_tensor", out=VS_[:, e8 * 128:(e8 + 1) * 128], in0=ps[:, bsp + e8 // 4, (e8 % 4) * 128:(e8 % 4) * 128 + 128], scalar=c(O_GSG + e8), in1=Bt[:, e8, :], op0=ALU.mult, op1=ALU.add),
                          (f"ps{bsp + e8 // 4}", "small", "Bt"), ((f"UB{18 + e8 // 2}",) if par == 0 else (f"VS2_{e8 // 2}",)))
                    V(R("tensor_tensor", out=su[:, :, tk], in0=VS_.rearrange("p (e i) -> p e i", i=128), in1=u_act[:, :, tk], op=ALU.mult),
                      (KVS,) + tuple(f"ysb{e}" for e in range(8)), tuple(f"sub{e}" for e in range(8)))
            if len(modq) > 0 and K_MODSPREAD:
                mod_block()
            S.tag = f"M5_t{t}"
            RGT = tuple(f"rgT{i}" for i in range(4))
            SU = tuple(f"su{i}" for i in range(4))
            for cb in range(2):
                sr = wload(w_ret, 0, 8, cb * 512, 512)
                ss = wload(w_sg, 0, 8, cb * 512, 512)
                sgr = wload(w_in, 0, 8, 5120 + cb * 512, 512)
                sgs = wload(w_in, 0, 8, 6144 + cb * 512, 512)
                for m in range(4):
                    e_ = cb * 4 + m
                    b1, b2, b3_, b4 = pb(), pb(), pb(), pb()
                    ms = slice(m * 128, (m + 1) * 128)
                    for k in range(8):
                        T(R("matmul", ps[:, b1, :], lhsT=Wt[sr][:, k, ms], rhs=rgT[:, k, :], start=(k == 0), stop=(k == 7)), (f"W{sr}",) + RGT, (f"ps{b1}",))
                    for k in range(8):
                        T(R("matmul", ps[:, b2, :], lhsT=Wt[ss][:, k, ms], rhs=su[:, k, :], start=(k == 0), stop=(k == 7)), (f"W{ss}", f"sub{k}"), (f"ps{b2}",))
                    for k in range(8):
                        T(R("matmul", ps[:, b3_, :], lhsT=Wt[sgr][:, k, ms], rhs=hT[:, k, :], start=(k == 0), stop=(k == 7)), (f"W{sgr}", hp + str(k)), (f"ps{b3_}",))
                    for k in range(8):
                        T(R("matmul", ps[:, b4, :], lhsT=Wt[sgs][:, k, ms], rhs=hT[:, k, :], start=(k == 0), stop=(k == 7)), (f"W{sgs}", hp + str(k)), (f"ps{b4}",))
                    A(R("activation", out=tmpf[0][:], in_=ps[:, b3_, :], func=AF.Sigmoid), (f"ps{b3_}",), ("tmpf0",))
                    A(R("activation", out=tmpf[1][:], in_=ps[:, b4, :], func=AF.Sigmoid), (f"ps{b4}",), ("tmpf1",))
                    V(R("tensor_tensor", out=tmpf[0][:], in0=tmpf[0][:], in1=ps[:, b1, :], op=ALU.mult), ("tmpf0", f"ps{b1}"), ("tmpf0",))
                    V(R("tensor_tensor", out=tmpf[1][:], in0=tmpf[1][:], in1=ps[:, b2, :], op=ALU.mult), ("tmpf1", f"ps{b2}"), ("tmpf1",))
                    V(R("tensor_tensor", out=ypre[:, e_, :], in0=tmpf[0][:], in1=tmpf[1][:], op=ALU.add), ("tmpf0", "tmpf1"), (f"ypre{e_}",))
            prefetch_norm()
            S.tag = f"M6_t{t}"
            out_proj_rows(w_o, 8, lambda kk: ypre[:, kk, :], lambda kk: (f"ypre{kk}",))
            post_norm(1, t)

        for t in range(3):
            ffn(0, t)
        while len(modq) > 6:
            mod_block()
        if not K_MODSPREAD:
            while modq:
                mod_block()
        mixer_M12(1, full=False)
        for t in range(3):
            mixer_M12(t, full=True)
            mixer_rest(t)
        def out_tile(t):
            S.tag = "out"
            UFv = UF[:, :].rearrange("p (c f) -> p c f", f=1024)
            allxs = tuple(f"xs{i}" for i in range(4))
            for k in range(8):
                b = pb()
                for c4 in range(4):
                    tok = slice((4 * t + c4) * 128, (4 * t + c4 + 1) * 128)
                    T(R("transpose", out=ps[:, b, c4 * 128:(c4 + 1) * 128], in_=x_all[:, k, tok], identity=c(O_ID, 128)), (xk(k, t), "small"), (f"ps{b}",))
                src = ps[:, b, :].rearrange("p (c q) -> p c q", q=128)
                if k % 2 == 0:
                    A(R("activation", out=UFv[:, :, k * 128:(k + 1) * 128], in_=src, func=AF.Copy), (f"ps{b}",), allxs)
                else:
                    V(R("tensor_copy", out=UFv[:, :, k * 128:(k + 1) * 128], in_=src), (f"ps{b}",), allxs)
            for c4 in range(4):
                c12 = 4 * t + c4
                S.out_dmas.append(LD(R("dma_start", out=y_out[c12 * 128:(c12 + 1) * 128, :], in_=xs[c4]), f"st_y{c4}", (), rd=(f"xs{c4}",)))

        while modq:
            mod_block()
        for t in range(3):
            ffn(2, t)
            out_tile(t)

        S.reorder({"pe": K_WIN[0], "act": K_WIN[1], "dve": K_WIN[2], "pool": 1, "sp": 4})
        S.resolve()
        _DBG["S"] = S
        sem_names = set()
        for e in S.ENGS:
            for op in S.ops[e]:
                if op.is_dma:
                    sem_names.add(op.sem)
        sems = {n: es.enter_context(nc.semaphore(n)) for n in sorted(sem_names)}
        esem = {e: es.enter_context(nc.semaphore("eng_" + e)) for e in ("pe", "act", "dve")}
        block = es.enter_context(nc.Block())

        def emit(eng_name):
            def body(e):
                for op in S.ops[eng_name]:
                    for (key, v) in op.waits:
                        e.wait_ge(sems[key[1]] if key[0] == "dma" else esem[key[1]], v)
                    ins = op.fn(e)
                    if op.is_dma:
                        ins.then_inc(sems[op.sem], 16)
                    elif op.inc:
                        ins.then_inc(esem[eng_name], 1)
                if eng_name == "sp":
                    final = {}
                    for op in S.out_dmas:
                        final[op.sem] = max(final.get(op.sem, 0), op.val)
                    for k, v in final.items():
                        e.wait_ge(sems[k], v)
            return body

        block.tensor(emit("pe"))
        block.scalar(emit("act"))
        block.vector(emit("dve"))
        block.gpsimd(emit("pool"))
        block.sync(emit("sp"))
    return nc


def _rope_tables(L):
    rows = L // 64
    r = np.repeat(np.arange(rows), 64).astype(np.float32)
    col = (np.arange(rows * 64) % 64).astype(np.float32)
    nf = 32
    freqs = (np.float32(10000.0) ** (-np.arange(nf, dtype=np.float32) / np.float32(nf))).astype(np.float32)
    ang = np.concatenate([r[:, None] * freqs, col[:, None] * freqs], axis=-1).astype(np.float32)
    return np.cos(ang).astype(np.float32), np.sin(ang).astype(np.float32)


def _core_layout(cid):
    if cid < 4:
        return cid, None, 2 * cid, 2 * cid + 1
    base = 8 + 6 * (cid - 4)
    return None, [base, base + 1, base + 2, base + 3], base + 4, base + 5


_NC_CACHE = {}
_DBG = {}


def kernel(x_prompt, x_sample, state_ret_fwd, state_ret_bwd, c, c_ctx, w_ada, b_ada, g_norm,
           w_ffn1_in, w_ffn1_out, w_in, ret_decay_logit, g_ret, w_ret_br, g_sg, b_sg, w_sp, b_sp,
           w_sg_br, w_out, w_ffn2_in, w_ffn2_out):
    f = lambda a: np.ascontiguousarray(np.asarray(a, dtype=np.float32))
    x_prompt, x_sample = f(x_prompt), f(x_sample)
    state_ret_fwd, state_ret_bwd = f(state_ret_fwd), f(state_ret_bwd)
    c, c_ctx = f(c), f(c_ctx)
    if "nc" not in _NC_CACHE:
        _NC_CACHE["nc"] = build_program()
    nc = _NC_CACHE["nc"]

    p = np.arange(128, dtype=np.float32)
    base_small = np.zeros((128, NS), np.float32)
    base_small[:, O_ID:O_ID + 128] = np.eye(128, dtype=np.float32)
    ii = np.arange(128, dtype=np.float32)
    base_small[:, O_A:O_A + 128] = np.maximum(ii[None, :] - p[:, None], 0.0)
    base_small[:, O_B:O_B + 128] = np.maximum(p[:, None] - ii[None, :], 0.0)
    base_small[:, O_R127:O_R127 + 128] = (127.0 - p)[:, None]
    base_small[:, O_RP:O_RP + 128] = p[:, None]
    base_small[:, O_RP1] = p + 1.0
    base_small[:, O_R128] = 128.0 - p
    base_small[:, O_EPS] = EPS
    base_small[:, O_ONE] = 1.0
    base_small[:, O_GN:O_GN + 48] = f(g_norm)[0].reshape(6, 8, 128).transpose(2, 0, 1).reshape(128, 48)
    base_small[:, O_BADA:O_BADA + 72] = f(b_ada)[0].reshape(72, 128).T
    base_small[:, O_GRET:O_GRET + 8] = f(g_ret)[0].reshape(8, 128).T
    base_small[:, O_GSG:O_GSG + 8] = f(g_sg)[0].reshape(8, 128).T
    base_small[:, O_BSG:O_BSG + 8] = f(b_sg)[0].reshape(8, 128).T

    rows = np.zeros((1, NR), np.float32)
    rows[0, R_LG:R_LG + 8] = f(ret_decay_logit)[0].reshape(8)
    rows[0, R_GSG:R_GSG + 1024] = f(g_sg)[0]
    rows[0, R_BSG:R_BSG + 1024] = f(b_sg)[0]
    rows[0, R_BSP:R_BSP + 512] = f(b_sp)[0].reshape(512)

    cos_s, sin_s = _rope_tables(1024)
    shared = {
        "rows": rows, "w_ada": f(w_ada)[0], "w_ffn1_in": f(w_ffn1_in)[0], "w_ffn2_in": f(w_ffn2_in)[0],
        "w_ffn1_out": f(w_ffn1_out)[0], "w_ffn2_out": f(w_ffn2_out)[0], "w_in": f(w_in)[0],
        "w_ret_br": f(w_ret_br)[0], "w_sg_br": f(w_sg_br)[0], "w_out": f(w_out)[0], "w_sp": f(w_sp)[0],
    }
    in_maps = []
    for cid in range(8):
        smp, aps, pb_, pc_ = _core_layout(cid)
        if smp is not None:
            xa = x_sample[smp]
            cA = c[smp]
        else:
            xa = np.concatenate([x_prompt[i] for i in aps], axis=0)
            cA = c_ctx
        x_in = np.ascontiguousarray(np.concatenate([xa, x_prompt[pb_], x_prompt[pc_]], axis=0))
        sm = base_small.copy()
        sm[:, O_FLAG] = 1.0 if smp is not None else 0.0
        c2 = np.stack([cA, c_ctx], axis=0)
        sm[:, O_C2:O_C2 + 16] = c2.reshape(2, 8, 128).transpose(2, 1, 0).reshape(128, 16)
        rope = np.zeros((3, 2, 128, 4, 64), np.float32)
        rope[:, 0] = 1.0
        s0 = np.zeros((2, 128, 1024), np.float32)
        if smp is not None:
            cs = cos_s.reshape(2, 4, 128, 64).transpose(0, 2, 1, 3)
            sn = sin_s.reshape(2, 4, 128, 64).transpose(0, 2, 1, 3)
            rope[0:2, 0] = cs
            rope[0:2, 1] = sn
            s0[0] = state_ret_fwd[smp, 0].transpose(1, 0, 2).reshape(128, 1024)
            s0[1] = state_ret_bwd[smp, 0].transpose(1, 0, 2).reshape(128, 1024)
        m = dict(shared)
        m.update({"x_in": x_in, "small": sm, "rope": rope, "s0": s0})
        in_maps.append(m)

    res = run_bass_kernel_spmd(nc, in_maps, core_ids=list(range(8)))
    y_prompt = np.zeros((32, 256, 1024), np.float32)
    y_sample = np.zeros((4, 1024, 1024), np.float32)
    nsf = np.zeros((32, 1, 4, 128, 256), np.float32)
    nsb = np.zeros((32, 1, 4, 128, 256), np.float32)

    def put_state(pidx, st, tile, pair):
        nsf[pidx, 0] = st[tile, pair, 0].reshape(128, 4, 256).transpose(1, 0, 2)
        nsb[pidx, 0] = st[tile, pair, 1].reshape(128, 4, 256).transpose(1, 0, 2)

    for cid in range(8):
        y = res.results[cid]["y_out"]
        st = res.results[cid]["st_out"]
        smp, aps, pb_, pc_ = _core_layout(cid)
        if smp is not None:
            y_sample[smp] = y[0:1024]
        else:
            for i, pi in enumerate(aps):
                y_prompt[pi] = y[i * 256:(i + 1) * 256]
                put_state(pi, st, i // 2, i % 2)
        y_prompt[pb_] = y[1024:1280]
        y_prompt[pc_] = y[1280:1536]
        put_state(pb_, st, 2, 0)
        put_state(pc_, st, 2, 1)
    return (y_prompt, y_sample, nsf, nsb)
```
